# Optimizing a Trainium2 kernel written in Bass

```python
import jax, jax.numpy as jnp
from jax import lax
import numpy as np

D_MODEL = 1024
BATCH = 8
SEQ = 2048
DEPTH = 2
DEC_BATCH = 128
DEC_SEQ = 4
PAST_LEN = 16384
PAGE_SIZE = 128

D_MIX = D_MODEL
D_RWKV = D_MIX // 2
HEAD_DIM = 64
N_HEADS_R = D_RWKV // HEAD_DIM
D_CONV = D_MIX - D_RWKV
N_CONV_GROUPS = 8
CONV_W = 31
LORA_W = 64
LORA_A = 64
P_RWKV = 4 * D_RWKV + LORA_W + LORA_A
P_CONV = 3 * D_CONV
P_IN = P_RWKV + P_CONV
RMS_EPS = 1e-6
RWKV_GN_EPS = 64e-5
CONV_GN_EPS = 1e-5

kernel_name = "hymba_rwkv7_conformer_conv_decode_step"


def _rms_norm(x, g):
    xf = x.astype(jnp.float32)
    y = xf * lax.rsqrt(jnp.mean(jnp.square(xf), axis=-1, keepdims=True) + RMS_EPS)
    return (y * g.astype(jnp.float32)).astype(x.dtype)


def _group_norm(x, n_groups, g, b, eps):
    xf = x.astype(jnp.float32)
    xg = xf.reshape(xf.shape[:-1] + (n_groups, xf.shape[-1] // n_groups))
    mean = jnp.mean(xg, axis=-1, keepdims=True)
    var = jnp.mean(jnp.square(xg - mean), axis=-1, keepdims=True)
    y = ((xg - mean) * lax.rsqrt(var + eps)).reshape(xf.shape)
    return y * g.astype(jnp.float32) + b.astype(jnp.float32)


def _heads(t):
    return t.reshape(t.shape[:-1] + (N_HEADS_R, HEAD_DIM))


def _wkv7(S0, r, decay, k, v, kk, a):
    def step(S, inp):
        r_t, w_t, k_t, v_t, kk_t, a_t = inp
        Sk = jnp.einsum('bhvk,bhk->bhv', S, kk_t)
        S = (S * w_t[:, :, None, :]
             - Sk[..., None] * (kk_t * a_t)[:, :, None, :]
             + v_t[..., None] * k_t[:, :, None, :])
        y_t = jnp.einsum('bhvk,bhk->bhv', S, r_t)
        return S, y_t
    xs = tuple(jnp.moveaxis(t, 1, 0) for t in (r, decay, k, v, kk, a))
    S, y = lax.scan(step, S0, xs)
    return jnp.moveaxis(y, 0, 1), S


def _layer(x, c, s_shift, s_wkv, s_conv,
           w_ada, b_ada, g_pre, g_post, w_in, mu, w0, w_up, a0, a_up, k_k, k_a, r_k,
           gn_r_g, gn_r_b, w_dw, b_dw, gn_c_g, gn_c_b, w_out):
    dt = x.dtype
    B, T, _ = x.shape
    mod = jax.nn.silu(c) @ w_ada + b_ada
    shift, scale, gate = jnp.split(mod, 3, axis=-1)
    h = _rms_norm(x, g_pre) * (1 + scale[:, None]) + shift[:, None]
    u = h @ w_in
    u_r, u_c = u[..., :P_RWKV], u[..., P_RWKV:]

    u_prev0 = s_shift.astype(dt) @ w_in[:, :P_RWKV]
    u_prev = jnp.concatenate([u_prev0[:, None], u_r[:, :-1]], axis=1)
    u_r = u_r + (u_prev - u_r) * mu
    r = u_r[..., 0 * D_RWKV:1 * D_RWKV]
    k = u_r[..., 1 * D_RWKV:2 * D_RWKV]
    v = u_r[..., 2 * D_RWKV:3 * D_RWKV]
    g_r = u_r[..., 3 * D_RWKV:4 * D_RWKV]
    w_lo = u_r[..., 4 * D_RWKV:4 * D_RWKV + LORA_W]
    a_lo = u_r[..., 4 * D_RWKV + LORA_W:]
    w_log = -jax.nn.softplus(-(w0 + jnp.tanh(w_lo) @ w_up).astype(jnp.float32)) - 0.5
    decay = jnp.exp(-jnp.exp(w_log))
    a = jax.nn.sigmoid((a0 + a_lo @ a_up).astype(jnp.float32))
    kf = k.astype(jnp.float32)
    kk = _heads(kf * k_k.astype(jnp.float32))
    kk = kk / jnp.maximum(jnp.sqrt(jnp.sum(jnp.square(kk), axis=-1, keepdims=True)), 1e-12)
    k_mod = kf * (1 + (a - 1) * k_a.astype(jnp.float32))
    rh = _heads(r.astype(jnp.float32))
    kh = _heads(k_mod)
    vh = _heads(v.astype(jnp.float32))
    y, S_new = _wkv7(s_wkv.astype(jnp.float32), rh, _heads(decay), kh, vh, kk, _heads(a))
    y = _group_norm(y.reshape(B, T, D_RWKV), N_HEADS_R, gn_r_g, gn_r_b, RWKV_GN_EPS)
    bonus = jnp.sum(rh * kh * r_k.astype(jnp.float32), axis=-1, keepdims=True) * vh
    y_r = (y + bonus.reshape(B, T, D_RWKV)) * jax.nn.silu(g_r.astype(jnp.float32))

    glu_a = u_c[..., :D_CONV]
    glu_b = u_c[..., D_CONV:2 * D_CONV]
    g_c = u_c[..., 2 * D_CONV:]
    glu = glu_a * jax.nn.sigmoid(glu_b)
    buf = jnp.concatenate([s_conv.astype(dt), glu], axis=1)
    conv = lax.conv_general_dilated(
        buf, w_dw[:, None, :], window_strides=(1,), padding='VALID',
        dimension_numbers=('NWC', 'WIO', 'NWC'), feature_group_count=D_CONV) + b_dw
    y_c = jax.nn.silu(_group_norm(conv, N_CONV_GROUPS, gn_c_g, gn_c_b, CONV_GN_EPS)) \
        * jax.nn.silu(g_c.astype(jnp.float32))

    mix = jnp.concatenate([y_r, y_c], axis=-1).astype(dt) @ w_out
    x = x + gate[:, None] * _rms_norm(mix, g_post)
    return x, h[:, -1], S_new, buf[:, -(CONV_W - 1):]


def _trunk(x, c, st_shift, st_wkv, st_conv, layer_w):
    shifts, wkvs, convs = [], [], []
    for l in range(DEPTH):
        x, s1, s2, s3 = _layer(x, c, st_shift[l], st_wkv[l], st_conv[l],
                               *[p[l] for p in layer_w])
        shifts.append(s1)
        wkvs.append(s2)
        convs.append(s3)
    return x, jnp.stack(shifts), jnp.stack(wkvs), jnp.stack(convs)


def setup_inputs(seed: int = 0) -> dict:
    key = jax.random.key(seed)
    ks = jax.random.split(key, 32)
    f = jnp.float32
    nrm = lambda k, shape, s: jax.random.normal(k, shape, f) * s
    return {
        "x_prompt": nrm(ks[0], (BATCH, SEQ, D_MODEL), 1.0),
        "x_sample": nrm(ks[1], (DEC_BATCH, DEC_SEQ, D_MODEL), 1.0),
        "c_prompt": nrm(ks[2], (BATCH, D_MODEL), 1.0),
        "c_sample": nrm(ks[3], (DEC_BATCH, D_MODEL), 1.0),
        "state_shift": nrm(ks[4], (DEPTH, DEC_BATCH, D_MODEL), 1.0),
        "state_wkv": nrm(ks[5], (DEPTH, DEC_BATCH, N_HEADS_R, HEAD_DIM, HEAD_DIM), 0.1),
        "state_conv": nrm(ks[6], (DEPTH, DEC_BATCH, CONV_W - 1, D_CONV), 0.5),
        "w_ada": nrm(ks[7], (DEPTH, D_MODEL, 3 * D_MODEL), 0.5 * D_MODEL ** -0.5),
        "b_ada": nrm(ks[8], (DEPTH, 3 * D_MODEL), 0.01),
        "g_pre": 1.0 + nrm(ks[9], (DEPTH, D_MODEL), 0.01),
        "g_post": 1.0 + nrm(ks[10], (DEPTH, D_MODEL), 0.01),
        "w_in": nrm(ks[11], (DEPTH, D_MODEL, P_IN), D_MODEL ** -0.5),
        "mu": jax.random.uniform(ks[12], (DEPTH, P_RWKV), f),
        "w0": jax.random.uniform(ks[13], (DEPTH, D_RWKV), f, -4.0, 1.0),
        "w_up": nrm(ks[14], (DEPTH, LORA_W, D_RWKV), 0.5 * LORA_W ** -0.5),
        "a0": nrm(ks[15], (DEPTH, D_RWKV), 0.1),
        "a_up": nrm(ks[16], (DEPTH, LORA_A, D_RWKV), LORA_A ** -0.5),
        "k_k": 0.85 + nrm(ks[17], (DEPTH, D_RWKV), 0.02),
        "k_a": 1.0 + nrm(ks[18], (DEPTH, D_RWKV), 0.02),
        "r_k": nrm(ks[19], (DEPTH, N_HEADS_R, HEAD_DIM), 0.1),
        "gn_r_g": 1.0 + nrm(ks[20], (DEPTH, D_RWKV), 0.01),
        "gn_r_b": nrm(ks[21], (DEPTH, D_RWKV), 0.01),
        "w_dw": nrm(ks[22], (DEPTH, CONV_W, D_CONV), CONV_W ** -0.5),
        "b_dw": nrm(ks[23], (DEPTH, D_CONV), 0.01),
        "gn_c_g": 1.0 + nrm(ks[24], (DEPTH, D_CONV), 0.01),
        "gn_c_b": nrm(ks[25], (DEPTH, D_CONV), 0.01),
        "w_out": nrm(ks[26], (DEPTH, D_MIX, D_MODEL), D_MIX ** -0.5),
    }


def reference(x_prompt, x_sample, c_prompt, c_sample, state_shift, state_wkv, state_conv,
              w_ada, b_ada, g_pre, g_post, w_in, mu, w0, w_up, a0, a_up, k_k, k_a, r_k,
              gn_r_g, gn_r_b, w_dw, b_dw, gn_c_g, gn_c_b, w_out):
    layer_w = (w_ada, b_ada, g_pre, g_post, w_in, mu, w0, w_up, a0, a_up, k_k, k_a, r_k,
               gn_r_g, gn_r_b, w_dw, b_dw, gn_c_g, gn_c_b, w_out)
    B = x_prompt.shape[0]
    z_shift = jnp.zeros((DEPTH, B, D_MODEL), x_prompt.dtype)
    z_wkv = jnp.zeros((DEPTH, B, N_HEADS_R, HEAD_DIM, HEAD_DIM), jnp.float32)
    z_conv = jnp.zeros((DEPTH, B, CONV_W - 1, D_CONV), x_prompt.dtype)
    y_prompt, shift_p, wkv_p, conv_p = _trunk(x_prompt, c_prompt, z_shift, z_wkv, z_conv, layer_w)
    y_sample, shift_s, wkv_s, conv_s = _trunk(x_sample, c_sample, state_shift, state_wkv,
                                              state_conv, layer_w)
    return (y_prompt, y_sample, shift_p, wkv_p, conv_p, shift_s, wkv_s, conv_s)
```

```python
import math
import numpy as np
import concourse.bass as bass
import concourse.mybir as mybir
from concourse.bass_utils import run_bass_kernel_spmd

F32 = mybir.dt.float32
BF16 = mybir.dt.bfloat16
AF = mybir.ActivationFunctionType
ALU = mybir.AluOpType
AX = mybir.AxisListType

D = 1024
T = 2048
NSEQ = 16
TS = 64
DEPTH = 2
PIN = 3712
NCH = 29
TT = 256
NT = T // TT
CH = 64
C0 = math.exp(-0.5)
NPRM = 189


class Res:
    __slots__ = ("w", "rd", "excl")

    def __init__(self):
        self.w = None
        self.rd = {}
        self.excl = False


class Sched:
    ENG = ("pe", "dve", "act", "pool", "sp")

    def __init__(self, nc, ndma=40):
        self.nc = nc
        self.prog = {e: [] for e in self.ENG}
        self.cnt = {e: 0 for e in self.ENG}
        self.waited = {e: {} for e in self.ENG}
        self.sems = {}
        self.ndma = ndma
        self.dtot = [0] * ndma
        self.drange = {"sp": (0, ndma - 8), "pool": (ndma - 8, ndma)}
        self.dnext = {"sp": 0, "pool": ndma - 8}

    def alloc(self, stack):
        for e in self.ENG:
            self.sems[("e", e)] = stack.enter_context(self.nc.semaphore("s_" + e))
        for k in range(self.ndma):
            self.sems[("d", k)] = stack.enter_context(self.nc.semaphore("d%d" % k))

    def _need(self, eng, tok):
        if tok is None:
            return
        kind, key, val = tok
        if kind == "e" and key == eng and eng in ("pe", "sp"):
            return
        cur = self.waited[eng].get((kind, key), 0)
        if cur >= val:
            return
        self.waited[eng][(kind, key)] = val
        self.prog[eng].append(("w", (kind, key), val))

    def _deps(self, eng, reads, writes):
        for r in reads:
            self._need(eng, r.w)
            if r.excl:
                for (kind, key), val in r.rd.items():
                    if kind == "e" and key == eng:
                        continue
                    self._need(eng, (kind, key, val))
        for w in writes:
            self._need(eng, w.w)
            for (kind, key), val in w.rd.items():
                self._need(eng, (kind, key, val))

    def _commit(self, tok, reads, writes):
        k2 = (tok[0], tok[1])
        for r in reads:
            if r.rd.get(k2, 0) < tok[2]:
                r.rd[k2] = tok[2]
        for w in writes:
            w.w = tok
            w.rd = {}

    def op(self, eng, fn, reads=(), writes=()):
        self._deps(eng, reads, writes)
        self.cnt[eng] += 1
        tok = ("e", eng, self.cnt[eng])
        self.prog[eng].append(("o", fn))
        self._commit(tok, reads, writes)

    def dma(self, q, out, in_, reads=(), writes=()):
        k = self.dnext[q]
        lo, hi = self.drange[q]
        self.dnext[q] = lo + (k + 1 - lo) % (hi - lo)
        self._deps(q, reads, writes)
        self._need(q, ("d", k, self.dtot[k]) if self.dtot[k] else None)
        self.dtot[k] += 16
        tok = ("d", k, self.dtot[k])
        self.prog[q].append(("d", out, in_, k))
        self._commit(tok, reads, writes)

    def barrier(self):
        for e in self.ENG:
            for f in self.ENG:
                if f != e and self.cnt[f]:
                    self._need(e, ("e", f, self.cnt[f]))
            for k in range(self.ndma):
                if self.dtot[k]:
                    self._need(e, ("d", k, self.dtot[k]))

    def emit(self, block, final_wait):
        nc = self.nc
        hand = {"pe": block.tensor, "dve": block.vector, "act": block.scalar,
                "pool": block.gpsimd, "sp": block.sync}

        def mk(e):
            def body(h):
                for it in self.prog[e]:
                    if it[0] == "w":
                        h.wait_ge(self.sems[it[1]], it[2])
                    elif it[0] == "o":
                        it[1](h).then_inc(self.sems[("e", e)], 1)
                    else:
                        h.dma_start(out=it[1], in_=it[2]).then_inc(self.sems[("d", it[3])], 16)
                if e == "sp":
                    for k in range(self.ndma):
                        if self.dtot[k]:
                            h.wait_ge(self.sems[("d", k)], self.dtot[k])
            return body

        for e in self.ENG:
            hand[e](mk(e))


class Tl:
    def __init__(self, t, nres=1):
        self.t = t
        self.res = [Res() for _ in range(nres)]

    def __getitem__(self, k):
        return self.t[k]

    @property
    def r(self):
        return self.res[0]


def host_consts():
    p = np.arange(128)
    hp, tp = p // 64, p % 64
    same_h = hp[:, None] == hp[None, :]
    jlt = tp[:, None] < tp[None, :]
    jle = tp[:, None] <= tp[None, :]
    sq = (tp[:, None] // 4) == (tp[None, :] // 4)
    ident = np.eye(128)
    isel = (tp[:, None] == np.arange(64)[None, :]).astype(np.float64)
    bones = same_h.astype(np.float64)

    def masks(extra):
        su = (same_h & jlt & extra).astype(np.float64)
        u = (same_h & jle & extra).astype(np.float64)
        sl = su.T
        mA = np.concatenate([su, u], 1)
        mB = -mA
        mC = np.concatenate([-sl, np.ones((128, 64)), np.ones((128, 128)), -np.ones((128, 128))], 1)
        return mA, mB, mC
    mAp, mBp, mCp = masks(np.ones((128, 128), bool))
    mAs, mBs, mCs = masks(sq)
    rowsel = ((tp[:, None] // 4) == np.arange(16)[None, :]).astype(np.float64)
    rp = np.ones((128, TT)); rp[:, ::CH] = 0
    rs = np.ones((128, TS)); rs[:, ::4] = 0
    cols = [ident, isel, rowsel, rp, rs, bones, mAp, mBp, mCp, mAs, mBs, mCs]
    offs = {}
    names = ["ident", "isel", "rowsel", "rp", "rs", "bones", "mAp", "mBp", "mCp", "mAs", "mBs", "mCs"]
    o = 0
    for n, c in zip(names, cols):
        offs[n] = (o, c.shape[1])
        o += c.shape[1]
    cst = np.concatenate(cols, 1).astype(np.float32)
    E = np.zeros((17, 192), np.float32)
    E[0, 0:128] = 1.0
    for s in range(16):
        E[1 + s, 128 + 4 * s:128 + 4 * s + 4] = 1.0
    return cst, offs, E


CST, COFF, CSTE = host_consts()
NCST = CST.shape[1]
NCSTF = 128 + 64 + 16 + TT + TS


def pack_params(inp, l):
    fm = lambda v, n: np.ascontiguousarray(np.asarray(v, np.float32).reshape(n, 128).T)
    cols = [fm(inp["mu"][l], 17), fm(inp["w0"][l], 4), fm(inp["a0"][l], 4), fm(inp["k_k"][l], 4),
            fm(inp["k_a"][l], 4), fm(inp["r_k"][l].reshape(-1), 4), fm(inp["gn_r_g"][l], 4),
            fm(inp["gn_r_b"][l], 4), fm(inp["b_dw"][l], 4), fm(inp["gn_c_g"][l], 4),
            fm(inp["gn_c_b"][l], 4)]
    wd = np.asarray(inp["w_dw"][l], np.float32)
    wdT = wd.T.reshape(4, 128, 31).transpose(1, 0, 2).reshape(128, 124)
    cols.append(wdT)
    cols.append(fm(inp["g_pre"][l], 8))
    return np.concatenate(cols, 1).astype(np.float32)


PO = {"mu": 0, "w0": 17, "a0": 21, "k_k": 25, "k_a": 29, "r_k": 33, "gn_r_g": 37, "gn_r_b": 41,
      "b_dw": 45, "gn_c_g": 49, "gn_c_b": 53, "wdw": 57, "g_pre": 181}


def build(dbg=None, stop=99, stop2=10**9):
    dbg = None
    from contextlib import ExitStack
    nc = bass.Bass("TRN2", target_bir_lowering=False)
    di = lambda n, s: nc.dram_tensor(n, list(s), F32, kind="ExternalInput").ap()
    do = lambda n, s: nc.dram_tensor(n, list(s), F32, kind="ExternalOutput").ap()
    xp_d = di("xp", (T, D)); xs_d = di("xs", (TS, D)); cc_d = di("cc", (17, D))
    ssh_d = di("ssh", (DEPTH, NSEQ, D)); swkv_d = di("swkv", (DEPTH, NSEQ, 8, 64, 64))
    scv_d = di("scv", (DEPTH, NSEQ * 30, 512))
    wada_d = di("w_ada", (DEPTH, D, 3 * D)); win_d = di("w_in", (DEPTH, D, PIN))
    wout_d = di("w_out", (DEPTH, D, D))
    prm_d = di("prm", (DEPTH, 128, NPRM)); prow_d = di("prow", (DEPTH, 17, 5 * D))
    lora_d = di("lora", (DEPTH, 128, 1024)); cst_d = di("cst", (128, NCST)); cste_d = di("cste", (17, 192))
    yp_d = do("yp", (T, D)); ys_d = do("ys", (TS, D)); shp_d = do("shp", (DEPTH, D))
    wkvp_d = do("wkvp", (DEPTH, 8, 64, 64)); cvp_d = do("cvp", (DEPTH, 30, 512))
    shs_d = do("shs", (DEPTH, NSEQ, D)); wkvs_d = do("wkvs", (DEPTH, NSEQ, 8, 64, 64))
    cvs_d = do("cvs", (DEPTH, NSEQ * 30, 512))
    xmid_d = yp_d
    dbg_d = do("dbg", (128, 8192)) if dbg else None

    with ExitStack() as st:
        S = Sched(nc)
        S.alloc(st)

        def sb(name, shape, dt=F32, nres=1, stack=st):
            return Tl(stack.enter_context(nc.sbuf_tensor("sb_" + name, list(shape), dt)), nres)

        def ps(name, shape, dt=F32):
            return Tl(st.enter_context(nc.psum_tensor(name, list(shape), dt)))

        V = lambda fn, r=(), w=(): S.op("dve", fn, r, w)
        A = lambda fn, r=(), w=(): S.op("act", fn, r, w)
        G = lambda fn, r=(), w=(): S.op("pool", fn, r, w)
        P = lambda fn, r=(), w=(): S.op("pe", fn, r, w)

        def MM(out, lhsT, rhs, r, w, start=True, stop=True):
            P(lambda e: e.matmul(out, lhsT, rhs, start=start, stop=stop), r, w)

        def MMG(lst, r, w):
            def fn(e):
                ins = None
                for (o, l, rh, s0, s1) in lst:
                    ins = e.matmul(o, l, rh, start=s0, stop=s1)
                return ins
            P(fn, r, w)

        def TR(out, in_, idn, r, w):
            P(lambda e: e.transpose(out, in_, idn), r, w)

        def act(out, in_, func, r, w, scale=1.0, bias=0.0, accum=None):
            if accum is None:
                A(lambda e: e.activation(out=out, in_=in_, func=func, bias=bias, scale=scale), r, w)
            else:
                A(lambda e: e.activation(out=out, in_=in_, func=func, bias=bias, scale=scale,
                                         accum_out=accum), r, w)

        def tt(eng, out, a, b, op, r, w):
            S.op(eng, lambda e: e.tensor_tensor(out, a, b, op), r, w)

        def ts(eng, out, a, s1, s2, op0, op1, r, w):
            if op1 is None:
                S.op(eng, lambda e: e.tensor_scalar(out, a, s1, None, op0), r, w)
            else:
                S.op(eng, lambda e: e.tensor_scalar(out, a, s1, s2, op0, op1), r, w)

        def stt(out, a, sc, b, op0, op1, r, w):
            V(lambda e: e.scalar_tensor_tensor(out, a, sc, b, op0, op1), r, w)

        dpos = [0]

        def dump(name, ap, res, n):
            if dbg_d is None:
                return
            o = dpos[0]
            dpos[0] += n
            print("DUMP", name, o, n)
            S.dma("sp", dbg_d[0:128, o:o + n], ap, [res], [])

        def sigmoid_from_exp(t, rr):
            ts("dve", t, t, 1.0, None, ALU.add, None, [rr], [rr])
            V(lambda e: e.reciprocal(t, t), [rr], [rr])

        cst = sb("cst", (128, NCSTF))
        cstb = sb("cstb", (128, NCST), BF16)
        cste = sb("cste", (17, 192))
        S.dma("sp", cst[:], cst_d[:, 0:NCSTF], [], [cst.r])
        S.dma("pool", cstb[:], cst_d[:, :], [], [cstb.r])
        S.dma("sp", cste[:], cste_d[:, :], [], [cste.r])
        cf = lambda n: cst[:, COFF[n][0]:COFF[n][0] + COFF[n][1]]
        cb = lambda n: cstb[:, COFF[n][0]:COFF[n][0] + COFF[n][1]]

        win = sb("win", (128, 8, PIN), BF16, nres=1)
        wout = sb("wout", (128, 8, D), BF16)
        prm = sb("prm", (128, NPRM))
        drv = sb("drv", (128, 64))
        lora = sb("lora", (128, 1024), BF16)
        gsT = sb("gsT", (128, 8, 17)); shT = sb("shT", (128, 8, 17))
        ggp = sb("ggp", (128, D)); ggs = sb("ggs", (TS, D))
        gss = sb("gss", (TS, D)); shs_t = sb("shs_t", (TS, D))
        cT = sb("cT", (128, 8, 17), BF16)
        xs = sb("xs", (TS, D))
        hlast = sb("hlast", (128, 8))
        PBD = []
        for q_ in range(2):
            d_ = {"KRbd": sb("KRbd%d" % q_, (128, 4, 2, 128), BF16), "bbd": sb("bbd%d" % q_, (128, 4, 128), BF16),
                  "kbd": sb("kbd%d" % q_, (128, 4, 128), BF16), "vbd": sb("vbd%d" % q_, (128, 4, 128), BF16),
                  "ynbd": sb("ynbd%d" % q_, (128, 128), BF16)}
            for tl in d_.values():
                G(lambda e, tl=tl: e.memset(tl[:], 0.0), [], [tl.r])
            PBD.append(d_)

        B = [ps("bk%d" % i, (128, 512)) for i in range(8)]
        for b_ in B:
            b_.r.excl = True

        rem_ = nc.sbuf_bytes_remaining
        NB = 13600
        NF = (rem_ - NB * 2) // 4 // 4 * 4 - 8
        print("SBUF remaining", rem_, "NF", NF, "NB", NB)
        arF = st.enter_context(nc.sbuf_tensor("arF", [128, NF], F32))
        arB = st.enter_context(nc.sbuf_tensor("arB", [128, NB], BF16))
        apos = {"f": 0, "b": 0}

        def areset():
            S.barrier()
            apos["f"] = 0
            apos["b"] = 0

        def ar(shape, dt=F32, nres=1):
            n = 1
            for d_ in shape[1:]:
                n *= d_
            key = "f" if dt == F32 else "b"
            base = arF if dt == F32 else arB
            o = apos[key]
            n2 = (n + 3) // 4 * 4
            apos[key] = o + n2
            assert apos[key] <= (NF if dt == F32 else NB), (key, apos[key])
            ap = base[0:shape[0], o:o + n]
            if len(shape) == 3:
                ap = ap.rearrange("p (a b) -> p a b", b=shape[2])
            elif len(shape) == 4:
                ap = ap.rearrange("p (a b c) -> p a b c", b=shape[2], c=shape[3])
            amax[key] = max(amax[key], apos[key])
            return Tl(ap, nres)

        amax = {"f": 0, "b": 0}

        class NS:
            pass
        Z = NS()

        def make_set(q_, W):
            bs = NS()
            bs.q = q_
            bs.bv = ar((128, W)); bs.sgr = ar((128, W)); bs.dec = ar((128, 16))
            bs.M1 = [ar((128, 256), BF16) for i in range(2)]
            bs.M2 = [ar((128, 256), BF16) for i in range(2)]
            bs.C4 = [ar((128, 448), BF16) for i in range(2)]
            bs.TmT = [ar((128, 128), BF16) for i in range(2)]
            bs.BmV = [ar((128, 64)) for i in range(2)]
            bs.XT = [ar((128, 3, 128), BF16) for i in range(2)]
            bs.Xb = ar((128, 64), BF16); bs.Ub = ar((128, 64), BF16)
            bs.y1 = ar((128, 64)); bs.yv = ar((128, 64)); bs.t64 = ar((128, 64)); bs.t64b = ar((128, 64))
            bs.st6 = ar((128, 6)); bs.mv = ar((128, 2)); bs.rs1 = ar((128, 2))
            for k_, v_ in PBD[q_].items():
                setattr(bs, k_, v_)
            bs.bk = [B[4], B[5], B[6], B[7]] if q_ == 0 else [B[0], B[1], B[2], B[3]]
            return bs

        def carve(W, sample):
            WP = W + (NSEQ if sample else 0)
            Z.hT = ar((128, 8, WP), BF16)
            Z.xt = ar((128, D))
            Z.xh = ar((128, 512))
            Z.o30 = Z.xh
            Z.junk = ar((128, 512), BF16)
            Z.sm = ar((128, 16))
            Z.Uc = [ar((128, WP + 1)) for i in range(2)]
            Z.Up = ar((128, TS)) if sample else None
            Z.lsc = ar((128, W))
            Z.urw = ar((128, 17, W), nres=17)
            Z.sgA = ar((128, 4, W), nres=4)
            Z.aA = ar((128, 4, W), nres=4)
            Z.lob = ar((128, W), BF16)
            Z.tmp = [ar((128, max(W, 128))) for i in range(8)]
            Z.tb = [ar((128, W), BF16) for i in range(2)]
            Z.dg = ar((128, 16, 128), BF16) if sample else None
            Z.ymix = ar((128, 8, W), BF16, nres=8)
            Z.sets = [make_set(0, W)] + ([] if sample else [make_set(1, W)])
            if sample:
                Z.Hsf = ar((128, 16, 64)); Z.Hsb = ar((128, 16, 64), BF16)
                Z.sw_in = ar((64, NSEQ, 2, 64), nres=NSEQ); Z.sw_out = ar((64, NSEQ, 2, 64), nres=NSEQ)
                Z.Gs = ar((128, 16, 64))
                Z.vsel = ar((128, 16, 64), BF16); Z.usel = ar((128, 16, 64), BF16)
                Z.cbs_f = ar((128, 4, 16, 34), nres=4)
                Z.cbs_b = ar((128, 4, 16, 34), BF16, nres=4)
                Z.scl = ar((120, 4, 512))
            else:
                Z.Hf = ar((128, 4, 64), nres=4); Z.Hb = ar((128, 4, 64), BF16, nres=4)
                Z.carry = ar((128, 17))
                Z.cbf = ar((128, 4, 30 + W), nres=4)
                Z.cacc = ar((128, 4, W), nres=4)
                Z.caccP = ar((128, W)); Z.ctmpP2 = [ar((128, W)) for i in range(2)]

        ident = cf("ident"); identb = cb("ident")
        bonesb = cb("bones"); iself = cf("isel"); iselb = cb("isel")

        def load_big(l):
            for kc in range(8):
                ws = Z.wst[kc % 2]
                S.dma("sp", ws[:, 0:PIN], win_d[l, kc * 128:(kc + 1) * 128, :], [], [ws.r])
                act(win[:, kc, 0:1280], ws[:, 0:1280], AF.Copy, [ws.r], [win.r])
                vcopy(win[:, kc, 1280:2560], ws[:, 1280:2560], [ws.r], [win.r])
                gcopy(win[:, kc, 2560:PIN], ws[:, 2560:PIN], [ws.r], [win.r])
            for k2 in range(4):
                ws = Z.wst[k2 % 2]
                for j in range(2):
                    kc = 2 * k2 + j
                    S.dma("sp", ws[:, j * D:(j + 1) * D], wout_d[l, kc * 128:(kc + 1) * 128, :], [], [ws.r])
                act(wout[:, 2 * k2, :], ws[:, 0:D], AF.Copy, [ws.r], [wout.r])
                vcopy(wout[:, 2 * k2 + 1, :], ws[:, D:2 * D], [ws.r], [wout.r])

        def load_layer(l):
            S.dma("sp", prm[:], prm_d[l], [], [prm.r])
            S.dma("pool", lora[:], lora_d[l], [], [lora.r])
            ts("dve", drv[:, 0:17], prm[:, 0:17], -1.0, 1.0, ALU.mult, ALU.add, [prm.r], [drv.r])
            ts("dve", drv[:, 17:25], prm[:, 17:25], -1.0, None, ALU.mult, None, [prm.r], [drv.r])
            ts("dve", drv[:, 25:29], prm[:, 29:33], -1.0, 1.0, ALU.mult, ALU.add, [prm.r], [drv.r])
            ts("dve", drv[:, 29:37], prm[:, 49:57], -1.0, None, ALU.mult, None, [prm.r], [drv.r])

        pcol = lambda name, i=0: prm[:, PO[name] + i:PO[name] + i + 1]
        dcol = lambda o: drv[:, o:o + 1]

        def prologue(l, first):
            modr = ar((17, 3 * D)); gsr = ar((17, D)); ggr = ar((17, D))
            wab = [ar((128, 3 * D), BF16) for i in range(2)]
            Z.wst = [ar((128, PIN)) for i in range(2)]
            if first:
                ccx, cce = gsr, ggr
                S.dma("sp", ccx[:], cc_d[:, :], [], [ccx.r])
                act(cce[:], ccx[:], AF.Exp, [ccx.r], [cce.r], scale=-1.0)
                sigmoid_from_exp(cce[:], cce.r)
                tt("dve", cce[:], cce[:], ccx[:], ALU.mult, [ccx.r, cce.r], [cce.r])
                for kc in range(8):
                    TR(B[0][:, kc * 17:(kc + 1) * 17], cce[0:17, kc * 128:(kc + 1) * 128],
                       ident[0:17, 0:17], [cce.r, cst.r], [B[0].r])
                V(lambda e: e.tensor_copy(cT[:].rearrange("p a b -> p (a b)"), B[0][:, 0:136]),
                  [B[0].r], [cT.r])
            S.dma("sp", modr[:], prow_d[l, :, 0:3 * D], [], [modr.r])
            S.dma("sp", gsr[:], prow_d[l, :, 3 * D:4 * D], [], [gsr.r])
            S.dma("sp", ggr[:], prow_d[l, :, 4 * D:5 * D], [], [ggr.r])
            for kc in range(8):
                wb = wab[kc % 2]
                ws = Z.wst[kc % 2]
                S.dma("sp", ws[:, 0:3 * D], wada_d[l, kc * 128:(kc + 1) * 128, :], [], [ws.r])
                act(wb[:, 0:1024], ws[:, 0:1024], AF.Copy, [ws.r], [wb.r])
                vcopy(wb[:, 1024:2048], ws[:, 1024:2048], [ws.r], [wb.r])
                gcopy(wb[:, 2048:3072], ws[:, 2048:3072], [ws.r], [wb.r])
                for nb in range(6):
                    MM(B[nb][0:17, :], cT[:, kc, :], wb[:, nb * 512:(nb + 1) * 512],
                       [cT.r, wb.r], [B[nb].r], start=(kc == 0), stop=(kc == 7))
            for nb in range(6):
                cs = slice(nb * 512, (nb + 1) * 512)
                tt("dve", modr[:, cs], B[nb][0:17, :], modr[:, cs], ALU.add, [B[nb].r, modr.r], [modr.r])
            stt(gsr[:], modr[:, D:2 * D], 1.0, gsr[:], ALU.add, ALU.mult, [modr.r, gsr.r], [gsr.r])
            tt("dve", ggr[:], modr[:, 2 * D:3 * D], ggr[:], ALU.mult, [modr.r, ggr.r], [ggr.r])
            for kc in range(8):
                TR(B[0][:, kc * 17:(kc + 1) * 17], gsr[0:17, kc * 128:(kc + 1) * 128],
                   ident[0:17, 0:17], [gsr.r, cst.r], [B[0].r])
                TR(B[1][:, kc * 17:(kc + 1) * 17], modr[0:17, kc * 128:(kc + 1) * 128],
                   ident[0:17, 0:17], [modr.r, cst.r], [B[1].r])
            V(lambda e: e.tensor_copy(gsT[:].rearrange("p a b -> p (a b)"), B[0][:, 0:136]), [B[0].r], [gsT.r])
            V(lambda e: e.tensor_copy(shT[:].rearrange("p a b -> p (a b)"), B[1][:, 0:136]), [B[1].r], [shT.r])
            for hf in range(2):
                cs = slice(hf * 512, (hf + 1) * 512)
                MM(B[2][:, :], cste[:, 0:128], ggr[:, cs], [cste.r, ggr.r], [B[2].r])
                V(lambda e, cs=cs: e.tensor_copy(ggp[:, cs], B[2][:, :]), [B[2].r], [ggp.r])
                MM(B[3][0:TS, :], cste[:, 128:192], ggr[:, cs], [cste.r, ggr.r], [B[3].r])
                V(lambda e, cs=cs: e.tensor_copy(ggs[:, cs], B[3][0:TS, :]), [B[3].r], [ggs.r])
                MM(B[4][0:TS, :], cste[:, 128:192], gsr[:, cs], [cste.r, gsr.r], [B[4].r])
                V(lambda e, cs=cs: e.tensor_copy(gss[:, cs], B[4][0:TS, :]), [B[4].r], [gss.r])
                MM(B[5][0:TS, :], cste[:, 128:192], modr[:, cs], [cste.r, modr.r], [B[5].r])
                V(lambda e, cs=cs: e.tensor_copy(shs_t[:, cs], B[5][0:TS, :]), [B[5].r], [shs_t.r])

        def rstd_from_ss(ss_ap, out_ap, scale, eps, r, w):
            act(out_ap, ss_ap, AF.Ln, r, w, scale=scale, bias=eps)
            act(out_ap, out_ap, AF.Exp, w, w, scale=-0.5)

        v3 = lambda ap: ap.rearrange("p (s t) -> p s t", t=4)

        def vcopy(out, in_, r, w):
            V(lambda e: e.tensor_copy(out, in_), r, w)

        def gcopy(out, in_, r, w):
            G(lambda e: e.tensor_copy(out, in_), r, w)

        def stage1(l, subs, sample, want_last):
            hT, xt, junk, sm, tmp, xh = Z.hT, Z.xt, Z.junk, Z.sm, Z.tmp, Z.xh
            nst = len(subs)
            W = nst * 128 if not sample else TS
            for si in range(nst):
                loader, xsrc, xres_r, npart = subs[si]
                if loader is not None:
                    loader(xt)
                    xsrc, xres_r = xt[0:npart, :], xt.r
                for hf in range(2):
                    cs = slice(hf * 512, (hf + 1) * 512)
                    act(junk[0:npart, :], xsrc[:, cs], AF.Square, [xres_r], [junk.r, sm.r],
                        accum=sm[0:npart, hf:hf + 1])
                tt("dve", sm[0:npart, 2:3], sm[0:npart, 0:1], sm[0:npart, 1:2], ALU.add, [sm.r], [sm.r])
                rstd_from_ss(sm[0:npart, 2:3], sm[0:npart, 3:4], 1.0 / D, 1e-6, [sm.r], [sm.r])
                ts("dve", xt[0:npart, :], xsrc, sm[0:npart, 3:4], None, ALU.mult, None, [xres_r, sm.r], [xt.r])
                if sample:
                    for hf in range(2):
                        cs = slice(hf * 512, (hf + 1) * 512)
                        tt("dve", xh[0:TS, :], xt[0:TS, cs], gss[:, cs], ALU.mult, [xt.r, gss.r], [xh.r])
                        tt("dve", xh[0:TS, :], xh[0:TS, :], shs_t[:, cs], ALU.add, [xh.r, shs_t.r], [xh.r])
                        for s_ in range(NSEQ):
                            S.dma("sp", shs_d[l, s_:s_ + 1, cs], xh[4 * s_ + 3:4 * s_ + 4, :], [xh.r], [])
                for kc in range(8):
                    bank = B[kc // 2]
                    col = (kc % 2) * 256 + si * 128
                    TR(bank[:, col:col + npart], xt[0:npart, kc * 128:(kc + 1) * 128],
                       ident[0:npart, 0:npart], [xt.r, cst.r], [bank.r])
            if sample:
                S.dma("sp", xt[0:NSEQ, :], ssh_d[l], [], [xt.r])
                for kc in range(8):
                    bank = B[kc // 2]
                    col = (kc % 2) * 256 + TS
                    TR(bank[:, col:col + NSEQ], xt[0:NSEQ, kc * 128:(kc + 1) * 128],
                       ident[0:NSEQ, 0:NSEQ], [xt.r, cst.r], [bank.r])
            for kc in range(8):
                bank = B[kc // 2]
                c0 = (kc % 2) * 256
                if not sample:
                    act(hT[:, kc, 0:W], bank[:, c0:c0 + W], AF.Identity, [bank.r, gsT.r, shT.r], [hT.r],
                        scale=gsT[:, kc, 0:1], bias=shT[:, kc, 0:1])
                    if want_last:
                        act(hlast[:, kc:kc + 1], bank[:, c0 + W - 1:c0 + W], AF.Identity,
                            [bank.r, gsT.r, shT.r], [hlast.r], scale=gsT[:, kc, 0:1], bias=shT[:, kc, 0:1])
                else:
                    tt("dve", v3(tmp[0][:, 0:TS]), v3(bank[:, c0:c0 + TS]),
                       gsT[:, kc, 1:17].unsqueeze(2).to_broadcast([128, 16, 4]), ALU.mult,
                       [bank.r, gsT.r], [tmp[0].r])
                    tt("dve", v3(hT[:, kc, 0:TS]), v3(tmp[0][:, 0:TS]),
                       shT[:, kc, 1:17].unsqueeze(2).to_broadcast([128, 16, 4]), ALU.add,
                       [tmp[0].r, shT.r], [hT.r])
                    vcopy(hT[:, kc, TS:TS + NSEQ], bank[:, c0 + TS:c0 + TS + NSEQ], [bank.r], [hT.r])
            if want_last:
                TR(B[2][0:8, 0:128], hlast[:, 0:8], ident, [hlast.r, cst.r], [B[2].r])
                vcopy(xh[0:8, 0:128], B[2][0:8, 0:128], [B[2].r], [xh.r])
                S.dma("sp", shp_d[l].rearrange("(c p) -> c p", p=128), xh[0:8, 0:128], [xh.r], [])

        def project(c, W, bank):
            hT = Z.hT
            MMG([(bank[:, 0:W], win[:, kc, c * 128:(c + 1) * 128], hT[:, kc, 0:W], kc == 0, kc == 7)
                 for kc in range(8)], [win.r, hT.r], [bank.r])

        def run_tasks(tasks):
            done = set()
            pending = list(tasks)
            active = []
            while pending or active:
                still = []
                for t_ in pending:
                    if all(d_ in done for d_ in t_[2]):
                        active.append((t_[0], t_[1]()))
                    else:
                        still.append(t_)
                pending = still
                assert active, "task deadlock"
                nxt = []
                for (n_, g_) in active:
                    try:
                        next(g_)
                        nxt.append((n_, g_))
                    except StopIteration:
                        done.add(n_)
                active = nxt

        def rwkv_tile(l, W, sample, extra_tasks=()):
            urw, sgA, aA, lob, tmp, tb, ymix, Uc = Z.urw, Z.sgA, Z.aA, Z.lob, Z.tmp, Z.tb, Z.ymix, Z.Uc
            WP = W + (NSEQ if sample else 0)
            mA, mB, mC = (cb("mAs"), cb("mBs"), cb("mCs")) if sample else (cb("mAp"), cb("mBp"), cb("mCp"))
            rmask = cf("rs") if sample else cf("rp")
            nchunk = W // CH
            nlev = 6 if not sample else 2
            lsc = Z.lsc
            pl_i = [0]

            def t_pl(chunks):
                for c in chunks:
                    i = pl_i[0]
                    pl_i[0] += 1
                    bank = B[2 + (i % 2)]
                    u = Uc[i % 2]
                    project(c, WP, bank)
                    if not sample:
                        carry = Z.carry
                        gcopy(u[:, 0:1], carry[:, c:c + 1], [carry.r], [u.r])
                        act(u[:, 1:1 + W], bank[:, 0:W], AF.Copy, [bank.r], [u.r])
                        gcopy(carry[:, c:c + 1], u[:, W:W + 1], [u.r], [carry.r])
                        tt("pool", lsc[:, 0:W], u[:, 0:W], u[:, 1:1 + W], ALU.subtract, [u.r], [lsc.r])
                        stt(urw[:, c, 0:W], lsc[:, 0:W], pcol("mu", c), u[:, 1:1 + W], ALU.mult, ALU.add,
                            [lsc.r, u.r, prm.r], [urw.res[c]])
                    else:
                        Up = Z.Up
                        act(u[:, 0:WP], bank[:, 0:WP], AF.Copy, [bank.r], [u.r])
                        vcopy(v3(Up[:, 0:TS])[:, :, 0:1], u[:, TS:TS + NSEQ].unsqueeze(2), [u.r], [Up.r])
                        vcopy(v3(Up[:, 0:TS])[:, :, 1:4], v3(u[:, 0:TS])[:, :, 0:3], [u.r], [Up.r])
                        tt("dve", lsc[:, 0:W], Up[:, 0:W], u[:, 0:W], ALU.subtract, [u.r, Up.r], [lsc.r])
                        stt(urw[:, c, 0:W], lsc[:, 0:W], pcol("mu", c), u[:, 0:W], ALU.mult, ALU.add,
                            [lsc.r, u.r, prm.r], [urw.res[c]])
                    yield

            def t_lora():
                e2 = tmp[1]
                act(e2[0:64, 0:W], urw[0:64, 16, 0:W], AF.Exp, [urw.res[16]], [e2.r], scale=2.0)
                ts("dve", e2[0:64, 0:W], e2[0:64, 0:W], 1.0, None, ALU.add, None, [e2.r], [e2.r])
                V(lambda e: e.reciprocal(e2[0:64, 0:W], e2[0:64, 0:W]), [e2.r], [e2.r])
                ts("dve", lob[0:64, 0:W], e2[0:64, 0:W], -2.0, 1.0, ALU.mult, ALU.add, [e2.r], [lob.r])
                vcopy(lob[64:128, 0:W], urw[64:128, 16, 0:W], [urw.res[16]], [lob.r])
                yield
                for fc in range(4):
                    MM(B[4][:, 0:W], lora[:, fc * 128:(fc + 1) * 128], lob[:, 0:W], [lora.r, lob.r], [B[4].r])
                    MM(B[5][:, 0:W], lora[:, 512 + fc * 128:512 + (fc + 1) * 128], lob[:, 0:W], [lora.r, lob.r], [B[5].r])
                    act(sgA[:, fc, 0:W], B[4][:, 0:W], AF.Exp, [B[4].r, drv.r], [sgA.res[fc]],
                        scale=-1.0, bias=dcol(17 + fc))
                    act(aA[:, fc, 0:W], B[5][:, 0:W], AF.Exp, [B[5].r, drv.r], [aA.res[fc]],
                        scale=-1.0, bias=dcol(21 + fc))
                    sigmoid_from_exp(sgA[:, fc, 0:W], sgA.res[fc])
                    sigmoid_from_exp(aA[:, fc, 0:W], aA.res[fc])
                    yield

            def t_convpre():
                cbf = Z.cbf
                for cc in range(4):
                    project(17 + cc, W, B[2]); project(21 + cc, W, B[3])
                    act(lsc[:, 0:W], B[3][:, 0:W], AF.Exp, [B[3].r], [lsc.r], scale=-1.0)
                    sigmoid_from_exp(lsc[:, 0:W], lsc.r)
                    tt("dve", cbf[:, cc, 30:30 + W], B[2][:, 0:W], lsc[:, 0:W], ALU.mult, [B[2].r, lsc.r], [cbf.res[cc]])
                    yield

            Lsg, Lx, eP, eN, ePx, kk, kmod, beta = [tmp[i] for i in range(8)]
            t1, ssc = Lsg, Lx
            w_ = lambda t: t[:, 0:W]
            c3 = lambda ap: ap.rearrange("p (c t) -> p c t", t=CH)

            def t_prep(fc, q):
                r_ = urw[:, fc, 0:W]; k_ = urw[:, 4 + fc, 0:W]; v_ = urw[:, 8 + fc, 0:W]; g_ = urw[:, 12 + fc, 0:W]
                rr, rk, rv, rg = urw.res[fc], urw.res[4 + fc], urw.res[8 + fc], urw.res[12 + fc]
                sg = sgA[:, fc, 0:W]; a_ = aA[:, fc, 0:W]
                bk0, bk1 = q.bk[0], q.bk[1]
                if sample:
                    Hsf, Hsb, sw_in = Z.Hsf, Z.Hsb, Z.sw_in
                    for s_ in range(NSEQ):
                        S.dma("sp", sw_in[:, s_, :, :], swkv_d[l, s_, 2 * fc:2 * fc + 2].rearrange("h v k -> v h k"),
                              [], [sw_in.res[s_]])
                    for s8 in range(0, NSEQ, 8):
                        for s_ in range(s8, s8 + 8):
                            TR(bk0[:, (s_ - s8) * 64:(s_ - s8 + 1) * 64],
                               sw_in[:, s_, :, :].rearrange("v h k -> v (h k)"),
                               ident[0:64, 0:64], [sw_in.res[s_], cst.r], [bk0.r])
                        vcopy(Hsf[:, s8:s8 + 8, :].rearrange("p s v -> p (s v)"), bk0[:, :], [bk0.r], [Hsf.r])
                        vcopy(Hsb[:, s8:s8 + 8, :].rearrange("p s v -> p (s v)"), bk0[:, :], [bk0.r], [Hsb.r])
                    yield
                V(lambda e, o_=w_(Lsg), m_=rmask[:, 0:W], s_in=sg: e.tensor_tensor_scan(
                    o_, m_, s_in, 0.0, ALU.mult, ALU.add), [sgA.res[fc], cst.r], [Lsg.r])
                tt("pool", w_(Lx), w_(Lsg), sg, ALU.subtract, [Lsg.r, sgA.res[fc]], [Lx.r])
                yield
                act(w_(eP), w_(Lsg), AF.Exp, [Lsg.r], [eP.r], scale=-C0)
                act(w_(eN), w_(Lsg), AF.Exp, [Lsg.r], [eN.r], scale=C0)
                act(w_(ePx), w_(Lx), AF.Exp, [Lx.r], [ePx.r], scale=-C0)
                yield
                ts("dve", w_(kk), k_, pcol("k_k", fc), None, ALU.mult, None, [rk, prm.r], [kk.r])
                act(tb[0][:, 0:W], w_(kk), AF.Square, [kk.r], [tb[0].r])
                MM(bk0[:, 0:W], bonesb, tb[0][:, 0:W], [cstb.r, tb[0].r], [bk0.r])
                yield
                ts("dve", w_(ssc), bk0[:, 0:W], 1e-24, None, ALU.max, None, [bk0.r], [ssc.r])
                act(w_(ssc), w_(ssc), AF.Ln, [ssc.r], [ssc.r])
                act(w_(ssc), w_(ssc), AF.Exp, [ssc.r], [ssc.r], scale=-0.5)
                tt("dve", w_(kk), w_(kk), w_(ssc), ALU.mult, [kk.r, ssc.r], [kk.r])
                yield
                ts("dve", w_(t1), a_, pcol("k_a", fc), dcol(25 + fc), ALU.mult, ALU.add,
                   [aA.res[fc], prm.r, drv.r], [t1.r])
                tt("pool", w_(kmod), k_, w_(t1), ALU.mult, [rk, t1.r], [kmod.r])
                tt("pool", w_(beta), w_(kk), a_, ALU.mult, [kk.r, aA.res[fc]], [beta.r])
                yield
                stt(tb[1][:, 0:W], r_, pcol("r_k", fc), w_(kmod), ALU.mult, ALU.mult,
                    [rr, prm.r, kmod.r], [tb[1].r])
                MM(bk1[:, 0:W], bonesb, tb[1][:, 0:W], [cstb.r, tb[1].r], [bk1.r])
                tt("dve", q.bv[:, 0:W], bk1[:, 0:W], v_, ALU.mult, [bk1.r, rv], [q.bv.r])
                yield
                act(q.sgr[:, 0:W], g_, AF.Exp, [rg], [q.sgr.r], scale=-1.0)
                sigmoid_from_exp(q.sgr[:, 0:W], q.sgr.r)
                tt("dve", q.sgr[:, 0:W], q.sgr[:, 0:W], g_, ALU.mult, [q.sgr.r, rg], [q.sgr.r])
                yield
                if not sample:
                    vcopy(q.dec[:, 0:nchunk], eP[:, CH - 1:W:CH], [eP.r], [q.dec.r])
                else:
                    vcopy(q.dec[:, 0:16], eP[:, 3:W:4], [eP.r], [q.dec.r])
                for hh in range(2):
                    pp = slice(hh * 64, (hh + 1) * 64)
                    tt("pool", q.KRbd[pp, 0:nchunk, 0, pp], c3(kk[pp, 0:W]), c3(ePx[pp, 0:W]), ALU.mult,
                       [kk.r, ePx.r], [q.KRbd.r])
                    tt("dve", q.KRbd[pp, 0:nchunk, 1, pp], c3(urw[pp, fc, 0:W]), c3(eP[pp, 0:W]), ALU.mult,
                       [rr, eP.r], [q.KRbd.r])
                    tt("pool", q.bbd[pp, 0:nchunk, pp], c3(beta[pp, 0:W]), c3(eN[pp, 0:W]), ALU.mult,
                       [beta.r, eN.r], [q.bbd.r])
                    tt("dve", q.kbd[pp, 0:nchunk, pp], c3(kmod[pp, 0:W]), c3(eN[pp, 0:W]), ALU.mult,
                       [kmod.r, eN.r], [q.kbd.r])
                    gcopy(q.vbd[pp, 0:nchunk, pp], c3(urw[pp, 8 + fc, 0:W]), [rv], [q.vbd.r])
                    yield

            def t_pre(fc, ch, q):
                b_ = ch % 2
                M1, M2, C4, TmT, BmV, XT = q.M1[b_], q.M2[b_], q.C4[b_], q.TmT[b_], q.BmV[b_], q.XT
                bkp, bkc, bkn, bkx = q.bk
                KR = q.KRbd[:, ch, :, :].rearrange("p a b -> p (a b)")
                kap = q.KRbd[:, ch, 0, :]
                MM(bkp[:, 0:256], q.kbd[:, ch, :], KR, [q.kbd.r, q.KRbd.r], [bkp.r])
                MM(bkp[:, 256:512], q.bbd[:, ch, :], KR, [q.bbd.r, q.KRbd.r], [bkp.r])
                MMG([(bkc[:, 0:128], kap, q.bbd[:, ch, :], True, True),
                     (bkc[:, 128:192], q.vbd[:, ch, :], iselb, True, True),
                     (bkc[:, 192:320], q.kbd[:, ch, :], identb, True, True),
                     (bkc[:, 320:448], q.bbd[:, ch, :], identb, True, True)],
                    [q.KRbd.r, q.bbd.r, q.vbd.r, q.kbd.r, cstb.r], [bkc.r])
                yield
                tt("dve", M1[:], bkp[:, 0:256], mA, ALU.mult, [bkp.r, cstb.r], [M1.r])
                tt("dve", M2[:], bkp[:, 256:512], mB, ALU.mult, [bkp.r, cstb.r], [M2.r])
                tt("dve", C4[:], bkc[:, 0:448], mC, ALU.mult, [bkc.r, cstb.r], [C4.r])
                yield
                BmT = M1[:, 0:128]
                Q0 = M2[:, 0:128]
                N0 = C4[:, 0:128]; Vtok = C4[:, 128:192]
                xa = XT[0]
                gcopy(xa[:, 0, :], Q0, [M2.r], [xa.r])
                tt("pool", xa[:, 1, :], Q0, identb, ALU.add, [M2.r, cstb.r], [xa.r])
                gcopy(xa[:, 2, :], N0, [C4.r], [xa.r])
                yield
                for lv in range(nlev - 1):
                    xb = XT[(lv + 1) % 2]
                    if lv == 0:
                        MMG([(bkn[:, 0:128], xa[:, 2, :], xa[:, 0, :], True, True),
                             (bkn[:, 256:384], xa[:, 0, :], xa[:, 2, :], True, True)], [xa.r], [bkn.r])
                        yield
                        act(xb[:, 0, :], bkn[:, 0:128], AF.Copy, [bkn.r], [xb.r])
                        act(xb[:, 2, :], bkn[:, 256:384], AF.Copy, [bkn.r], [xb.r])
                        gcopy(xb[:, 1, :], xa[:, 1, :], [xa.r], [xb.r])
                    else:
                        MMG([(bkn[:, 0:256], xa[:, 2, :], xa[:, 0:2, :].rearrange("p a b -> p (a b)"), True, True),
                             (bkn[:, 256:384], xa[:, 0, :], xa[:, 2, :], True, True)], [xa.r], [bkn.r])
                        yield
                        act(xb[:, 0, :], bkn[:, 0:128], AF.Copy, [bkn.r], [xb.r])
                        act(xb[:, 2, :], bkn[:, 256:384], AF.Copy, [bkn.r], [xb.r])
                        tt("dve", xb[:, 1, :], bkn[:, 128:256], xa[:, 1, :], ALU.add, [bkn.r, xa.r], [xb.r])
                    yield
                    xa = xb
                MM(bkn[:, 0:128], xa[:, 2, :], xa[:, 1, :], [xa.r], [bkn.r])
                yield
                tt("dve", TmT[:], bkn[:, 0:128], xa[:, 1, :], ALU.add, [bkn.r, xa.r], [TmT.r])
                MM(bkn[:, 384:448], BmT, Vtok, [M1.r, C4.r], [bkn.r])
                yield
                vcopy(BmV[:], bkn[:, 384:448], [bkn.r], [BmV.r])
                yield

            def t_chain(fc, ch, q):
                b_ = ch % 2
                M1, M2, C4, TmT, BmV = q.M1[b_], q.M2[b_], q.C4[b_], q.TmT[b_], q.BmV[b_]
                Xb, Ub, y1, yv, t64, t64b, st6, mv, rs1 = q.Xb, q.Ub, q.y1, q.yv, q.t64, q.t64b, q.st6, q.mv, q.rs1
                bkx = q.bk[3]
                cs = slice(ch * CH, (ch + 1) * CH)
                kap = q.KRbd[:, ch, 0, :]; rti = q.KRbd[:, ch, 1, :]
                CmT = M1[:, 128:256]; nDmT = M2[:, 128:256]
                Vtok = C4[:, 128:192]; kT = C4[:, 192:320]; nbT = C4[:, 320:448]
                if not sample:
                    Hf, Hb = Z.Hf, Z.Hb
                    hb = Hb[:, fc, :]
                    MMG([(bkx[:, 0:64], kap, hb, True, True),
                         (bkx[:, 192:256], rti, hb, True, False),
                         (bkx[:, 192:256], CmT, Vtok, False, True)],
                        [q.KRbd.r, Hb.res[fc], M1.r, C4.r], [bkx.r])
                    yield
                    tt("dve", Xb[:], bkx[:, 0:64], BmV[:], ALU.add, [bkx.r, BmV.r], [Xb.r])
                    act(y1[:], bkx[:, 192:256], AF.Copy, [bkx.r], [y1.r])
                    yield
                else:
                    Hsf, Hsb, Gs = Z.Hsf, Z.Hsb, Z.Gs
                    hall = Hsb[:].rearrange("p s v -> p (s v)")
                    rsel = cf("rowsel").unsqueeze(2).to_broadcast([128, 16, 64])
                    for (lh, dstt) in ((kap, t64), (rti, y1)):
                        for hf in range(2):
                            MM(B[hf][:, :], lh, hall[:, hf * 512:(hf + 1) * 512], [q.KRbd.r, Hsb.r], [B[hf].r])
                        for hf in range(2):
                            tt("dve", Gs[:, hf * 8:(hf + 1) * 8, :],
                               B[hf][:, :].rearrange("p (s v) -> p s v", v=64),
                               rsel[:, hf * 8:(hf + 1) * 8, :], ALU.mult, [B[hf].r, cst.r], [Gs.r])
                        V(lambda e, dstt=dstt: e.tensor_reduce(dstt[:], Gs[:].rearrange("p s v -> p v s"), AX.X, ALU.add),
                          [Gs.r], [dstt.r])
                    tt("dve", Xb[:], t64[:], BmV[:], ALU.add, [t64.r, BmV.r], [Xb.r])
                    MM(bkx[:, 192:256], CmT, Vtok, [M1.r, C4.r], [bkx.r])
                    tt("dve", y1[:], y1[:], bkx[:, 192:256], ALU.add, [y1.r, bkx.r], [y1.r])
                    yield
                MM(bkx[:, 64:128], TmT[:], Xb[:], [TmT.r, Xb.r], [bkx.r])
                yield
                act(Ub[:], bkx[:, 64:128], AF.Copy, [bkx.r], [Ub.r])
                yield
                if not sample:
                    MMG([(bkx[:, 256:320], nDmT, Ub[:], True, True),
                         (bkx[:, 128:192], kT, Vtok, True, False),
                         (bkx[:, 128:192], nbT, Ub[:], False, True)], [M2.r, C4.r, Ub.r], [bkx.r])
                    yield
                    tt("dve", t64[:], bkx[:, 128:192], Hf[:, fc, :], ALU.add, [bkx.r, Hf.res[fc]], [t64.r])
                    dec = q.dec[:, ch:ch + 1]
                    act(Hb[:, fc, :], t64[:], AF.Identity, [t64.r, q.dec.r], [Hb.res[fc]], scale=dec)
                    ts("dve", Hf[:, fc, :], t64[:], dec, None, ALU.mult, None, [t64.r, q.dec.r], [Hf.res[fc]])
                    tt("dve", yv[:], bkx[:, 256:320], y1[:], ALU.add, [bkx.r, y1.r], [yv.r])
                    yield
                else:
                    MM(bkx[:, 256:320], nDmT, Ub[:], [M2.r, Ub.r], [bkx.r])
                    tt("dve", yv[:], bkx[:, 256:320], y1[:], ALU.add, [bkx.r, y1.r], [yv.r])
                    vsel, usel, sw_out = Z.vsel, Z.usel, Z.sw_out
                    rsel = cf("rowsel").unsqueeze(2).to_broadcast([128, 16, 64])
                    tt("dve", vsel[:], Vtok.unsqueeze(1).to_broadcast([128, 16, 64]), rsel, ALU.mult,
                       [C4.r, cst.r], [vsel.r])
                    tt("dve", usel[:], Ub[:].unsqueeze(1).to_broadcast([128, 16, 64]), rsel, ALU.mult,
                       [Ub.r, cst.r], [usel.r])
                    for hf in range(2):
                        MMG([(B[hf][:, :], kT, vsel[:, hf * 8:(hf + 1) * 8, :].rearrange("p s v -> p (s v)"), True, False),
                             (B[hf][:, :], nbT, usel[:, hf * 8:(hf + 1) * 8, :].rearrange("p s v -> p (s v)"), False, True)],
                            [C4.r, vsel.r, usel.r], [B[hf].r])
                    decs = q.dec[:, 0:16].unsqueeze(2).to_broadcast([128, 16, 64])
                    for hf in range(2):
                        tt("dve", Hsf[:, hf * 8:(hf + 1) * 8, :], Hsf[:, hf * 8:(hf + 1) * 8, :],
                           B[hf][:, :].rearrange("p (s v) -> p s v", v=64), ALU.add,
                           [B[hf].r, Hsf.r], [Hsf.r])
                    tt("dve", Hsf[:], Hsf[:], decs, ALU.mult, [Hsf.r, q.dec.r], [Hsf.r])
                    for qq in range(4):
                        for j in range(4):
                            TR(B[4][0:64, j * 128:(j + 1) * 128], Hsf[:, 4 * qq + j, :], ident,
                               [Hsf.r, cst.r], [B[4].r])
                        vcopy(sw_out[:, 4 * qq:4 * qq + 4, :, :].rearrange("v s h k -> v s (h k)"),
                              B[4][0:64, :].rearrange("v (s x) -> v s x", x=128), [B[4].r], sw_out.res[4 * qq:4 * qq + 4])
                        for s_ in range(4 * qq, 4 * qq + 4):
                            S.dma("sp", wkvs_d[l, s_, 2 * fc:2 * fc + 2].rearrange("h v k -> v h k"),
                                  sw_out[:, s_, :, :], [sw_out.res[s_]], [])
                    yield
                V(lambda e: e.bn_stats(st6[:], yv[:]), [yv.r], [st6.r])
                V(lambda e: e.bn_aggr(mv[:], st6[:]), [st6.r], [mv.r])
                yield
                rstd_from_ss(mv[:, 1:2], rs1[:, 0:1], 1.0, 64e-5, [mv.r], [rs1.r])
                yield
                for hh in range(2):
                    pp = slice(hh * 64, (hh + 1) * 64)
                    ts("dve", q.ynbd[pp, pp], yv[pp, :], mv[pp, 0:1], rs1[pp, 0:1], ALU.subtract, ALU.mult,
                       [yv.r, mv.r, rs1.r], [q.ynbd.r])
                MM(bkx[:, 384:448], q.ynbd[:], iselb, [q.ynbd.r, cstb.r], [bkx.r])
                yield
                act(t64b[:], bkx[:, 384:448], AF.Identity, [bkx.r, prm.r], [t64b.r],
                    scale=pcol("gn_r_g", fc), bias=pcol("gn_r_b", fc))
                yield
                tt("pool", t64b[:], t64b[:], q.bv[:, cs], ALU.add, [t64b.r, q.bv.r], [t64b.r])
                tt("pool", ymix[:, fc, cs], t64b[:], q.sgr[:, cs], ALU.mult, [t64b.r, q.sgr.r], [ymix.res[fc]])
                yield

            tasks = []
            nset = len(Z.sets)
            tasks.append(("pl16", (lambda: t_pl([16])), []))
            tasks.append(("lora", t_lora, ["pl16"]))
            prev = "lora"
            for fc in range(4):
                tasks.append(("plp%d" % fc, (lambda fc=fc: t_pl([fc, 4 + fc, 8 + fc, 12 + fc])), [prev]))
                prev = "plp%d" % fc
            if not sample:
                tasks.append(("cpre", t_convpre, [prev]))
                prev = "cpre"
            plast = prev
            for fc in range(4):
                q = Z.sets[fc % nset]
                deps = ["plp%d" % fc, "lora"] if not sample else [plast]
                if fc >= 1:
                    deps.append("prep%d" % (fc - 1))
                if fc >= nset:
                    deps.append("chain%d_%d" % (fc - nset, nchunk - 1))
                tasks.append(("prep%d" % fc, (lambda fc=fc, q=q: t_prep(fc, q)), deps))
                for ch in range(nchunk):
                    d1 = ["prep%d" % fc]
                    if q.q == 1:
                        d1.append(plast)
                    if ch >= 1:
                        d1.append("pre%d_%d" % (fc, ch - 1))
                    if ch >= 2:
                        d1.append("chain%d_%d" % (fc, ch - 2))
                    tasks.append(("pre%d_%d" % (fc, ch), (lambda fc=fc, ch=ch, q=q: t_pre(fc, ch, q)), d1))
                    d2 = ["pre%d_%d" % (fc, ch)]
                    if ch >= 1:
                        d2.append("chain%d_%d" % (fc, ch - 1))
                    tasks.append(("chain%d_%d" % (fc, ch), (lambda fc=fc, ch=ch, q=q: t_chain(fc, ch, q)), d2))
            run_tasks(tasks + list(extra_tasks))

        def conv_tile(l, W, sample):
            tmp, tb, dg, ymix = Z.tmp, Z.tb, Z.dg, Z.ymix
            for cc in range(4):
                project(17 + cc, W, B[2]); project(21 + cc, W, B[3])
                eb = tmp[0]
                act(eb[:, 0:W], B[3][:, 0:W], AF.Exp, [B[3].r], [eb.r], scale=-1.0)
                sigmoid_from_exp(eb[:, 0:W], eb.r)
                if not sample:
                    cbf, cbb = Z.cbf, Z.cbb
                    tt("dve", cbf[:, cc, 30:30 + W], B[2][:, 0:W], eb[:, 0:W], ALU.mult, [B[2].r, eb.r], [cbf.res[cc]])
                    gcopy(cbb[:, cc, 30:30 + W], cbf[:, cc, 30:30 + W], [cbf.res[cc]], [cbb.res[cc]])
                else:
                    cbs_f, cbs_b = Z.cbs_f, Z.cbs_b
                    tt("dve", cbs_f[:, cc, :, 30:34], v3(B[2][:, 0:W]), v3(eb[:, 0:W]), ALU.mult,
                       [B[2].r, eb.r], [cbs_f.res[cc]])
                    vcopy(cbs_b[:, cc, :, :], cbs_f[:, cc, :, :], [cbs_f.res[cc]], [cbs_b.res[cc]])
                for (j0, j1) in ((0, 16), (16, 31)):
                    nj = j1 - j0
                    wd = prm[:, PO["wdw"] + cc * 31 + j0:PO["wdw"] + cc * 31 + j1]
                    tt("dve", dg[:, 0:nj, :], identb.unsqueeze(1).to_broadcast([128, nj, 128]),
                       wd.unsqueeze(2).to_broadcast([128, nj, 128]), ALU.mult, [cstb.r, prm.r], [dg.r])
                    if not sample:
                        MMG([(B[6][:, 0:W], dg[:, j - j0, :], cbb[:, cc, j:j + W], j == 0, j == 30) for j in range(j0, j1)],
                            [dg.r, cbb.res[cc]], [B[6].r])
                    else:
                        MMG([(v3(B[6][:, 0:W]), dg[:, j - j0, :], cbs_b[:, cc, :, j:j + 4], j == 0, j == 30)
                             for j in range(j0, j1)], [dg.r, cbs_b.res[cc]], [B[6].r])
                cf_, cb_, sq_ = tmp[1], tb[0], tb[1]
                act(cf_[:, 0:W], B[6][:, 0:W], AF.Identity, [B[6].r, prm.r], [cf_.r], bias=pcol("b_dw", cc))
                act(cb_[:, 0:W], B[6][:, 0:W], AF.Identity, [B[6].r, prm.r], [cb_.r], bias=pcol("b_dw", cc))
                act(sq_[:, 0:W], B[6][:, 0:W], AF.Square, [B[6].r, prm.r], [sq_.r], bias=pcol("b_dw", cc))
                MM(B[4][:, 0:W], bonesb, cb_[:, 0:W], [cstb.r, cb_.r], [B[4].r])
                MM(B[5][:, 0:W], bonesb, sq_[:, 0:W], [cstb.r, sq_.r], [B[5].r])
                msq, var, xc = tmp[2], tmp[3], tmp[4]
                act(msq[:, 0:W], B[4][:, 0:W], AF.Square, [B[4].r], [msq.r], scale=1.0 / 64)
                stt(var[:, 0:W], B[5][:, 0:W], 1.0 / 64, msq[:, 0:W], ALU.mult, ALU.subtract, [B[5].r, msq.r], [var.r])
                rstd_from_ss(var[:, 0:W], var[:, 0:W], 1.0, 1e-5, [var.r], [var.r])
                stt(xc[:, 0:W], B[4][:, 0:W], -1.0 / 64, cf_[:, 0:W], ALU.mult, ALU.add, [B[4].r, cf_.r], [xc.r])
                tt("pool", xc[:, 0:W], xc[:, 0:W], var[:, 0:W], ALU.mult, [xc.r, var.r], [xc.r])
                gnv, en = tmp[5], tmp[6]
                act(gnv[:, 0:W], xc[:, 0:W], AF.Identity, [xc.r, prm.r], [gnv.r],
                    scale=pcol("gn_c_g", cc), bias=pcol("gn_c_b", cc))
                act(en[:, 0:W], xc[:, 0:W], AF.Exp, [xc.r, drv.r], [en.r], scale=dcol(29 + cc), bias=dcol(33 + cc))
                sigmoid_from_exp(en[:, 0:W], en.r)
                tt("pool", gnv[:, 0:W], gnv[:, 0:W], en[:, 0:W], ALU.mult, [gnv.r, en.r], [gnv.r])
                project(25 + cc, W, B[2])
                eg = tmp[7]
                act(eg[:, 0:W], B[2][:, 0:W], AF.Exp, [B[2].r], [eg.r], scale=-1.0)
                sigmoid_from_exp(eg[:, 0:W], eg.r)
                tt("dve", eg[:, 0:W], eg[:, 0:W], B[2][:, 0:W], ALU.mult, [eg.r, B[2].r], [eg.r])
                tt("pool", ymix[:, 4 + cc, 0:W], gnv[:, 0:W], eg[:, 0:W], ALU.mult, [gnv.r, eg.r], [ymix.res[4 + cc]])
                if not sample:
                    gcopy(cbf[:, cc, 0:30], cbf[:, cc, W:W + 30], [cbf.res[cc]], [cbf.res[cc]])
                    gcopy(cbb[:, cc, 0:30], cbb[:, cc, W:W + 30], [cbb.res[cc]], [cbb.res[cc]])

        def conv_pre(l, W):
            tmp, cbf = Z.tmp, Z.cbf
            for cc in range(4):
                project(17 + cc, W, B[2]); project(21 + cc, W, B[3])
                eb = tmp[0]
                act(eb[:, 0:W], B[3][:, 0:W], AF.Exp, [B[3].r], [eb.r], scale=-1.0)
                sigmoid_from_exp(eb[:, 0:W], eb.r)
                tt("dve", cbf[:, cc, 30:30 + W], B[2][:, 0:W], eb[:, 0:W], ALU.mult, [B[2].r, eb.r], [cbf.res[cc]])

        def conv_tasks(l, W):
            cbf, cacc, accP, tmpP = Z.cbf, Z.cacc, Z.caccP, Z.ctmpP2
            wcol = lambda cc, j: prm[:, PO["wdw"] + cc * 31 + j:PO["wdw"] + cc * 31 + j + 1]

            NDV = 13

            def t_conv(cc):
                accD = cacc[:, cc, 0:W]
                rD = cacc.res[cc]
                ts("dve", accD, cbf[:, cc, 0:W], wcol(cc, 0), pcol("b_dw", cc), ALU.mult, ALU.add,
                   [cbf.res[cc], prm.r], [rD])
                act(accP[:, 0:W], cbf[:, cc, NDV:NDV + W], AF.Identity, [cbf.res[cc], prm.r], [accP.r],
                    scale=wcol(cc, NDV))
                yield
                jd, jp = 1, NDV + 1
                k_ = 0
                while jd < NDV or jp <= 30:
                    if jd < NDV:
                        stt(accD, cbf[:, cc, jd:jd + W], wcol(cc, jd), accD, ALU.mult, ALU.add,
                            [cbf.res[cc], prm.r, rD], [rD])
                        jd += 1
                    if jp <= 30:
                        tb_ = tmpP[k_ % 2]
                        act(tb_[:, 0:W], cbf[:, cc, jp:jp + W], AF.Identity, [cbf.res[cc], prm.r], [tb_.r],
                            scale=wcol(cc, jp))
                        tt("pool", accP[:, 0:W], accP[:, 0:W], tb_[:, 0:W], ALU.add, [accP.r, tb_.r], [accP.r])
                        jp += 1
                        k_ += 1
                    yield
                tt("pool", accD, accD, accP[:, 0:W], ALU.add, [rD, accP.r], [rD])
                yield
            tasks = []
            for cc in range(4):
                tasks.append(("conv%d" % cc, (lambda cc=cc: t_conv(cc)), (["conv%d" % (cc - 1)] if cc else ["cpre"])))
            return tasks

        def conv_post(l, W):
            tmp, tb, ymix, cbf, cacc = Z.tmp, Z.tb, Z.ymix, Z.cbf, Z.cacc
            for cc in range(4):
                cf_ = cacc[:, cc, 0:W]
                rC = cacc.res[cc]
                cb_, sq_ = tb[0], tb[1]
                act(cb_[:, 0:W], cf_, AF.Copy, [rC], [cb_.r])
                act(sq_[:, 0:W], cf_, AF.Square, [rC], [sq_.r])
                MM(B[4][:, 0:W], bonesb, cb_[:, 0:W], [cstb.r, cb_.r], [B[4].r])
                MM(B[5][:, 0:W], bonesb, sq_[:, 0:W], [cstb.r, sq_.r], [B[5].r])
                msq, var, xc = tmp[2], tmp[3], tmp[4]
                act(msq[:, 0:W], B[4][:, 0:W], AF.Square, [B[4].r], [msq.r], scale=1.0 / 64)
                stt(var[:, 0:W], B[5][:, 0:W], 1.0 / 64, msq[:, 0:W], ALU.mult, ALU.subtract, [B[5].r, msq.r], [var.r])
                rstd_from_ss(var[:, 0:W], var[:, 0:W], 1.0, 1e-5, [var.r], [var.r])
                stt(xc[:, 0:W], B[4][:, 0:W], -1.0 / 64, cf_, ALU.mult, ALU.add, [B[4].r, rC], [xc.r])
                tt("pool", xc[:, 0:W], xc[:, 0:W], var[:, 0:W], ALU.mult, [xc.r, var.r], [xc.r])
                gnv, en = tmp[5], tmp[6]
                act(gnv[:, 0:W], xc[:, 0:W], AF.Identity, [xc.r, prm.r], [gnv.r],
                    scale=pcol("gn_c_g", cc), bias=pcol("gn_c_b", cc))
                act(en[:, 0:W], xc[:, 0:W], AF.Exp, [xc.r, drv.r], [en.r], scale=dcol(29 + cc), bias=dcol(33 + cc))
                sigmoid_from_exp(en[:, 0:W], en.r)
                tt("pool", gnv[:, 0:W], gnv[:, 0:W], en[:, 0:W], ALU.mult, [gnv.r, en.r], [gnv.r])
                project(25 + cc, W, B[2])
                eg = tmp[7]
                act(eg[:, 0:W], B[2][:, 0:W], AF.Exp, [B[2].r], [eg.r], scale=-1.0)
                sigmoid_from_exp(eg[:, 0:W], eg.r)
                tt("dve", eg[:, 0:W], eg[:, 0:W], B[2][:, 0:W], ALU.mult, [eg.r, B[2].r], [eg.r])
                tt("pool", ymix[:, 4 + cc, 0:W], gnv[:, 0:W], eg[:, 0:W], ALU.mult, [gnv.r, eg.r], [ymix.res[4 + cc]])
                gcopy(cbf[:, cc, 0:30], cbf[:, cc, W:W + 30], [cbf.res[cc]], [cbf.res[cc]])

        def outproj(l, sample, subs, store):
            junk, sm, xt, xh, ymix = Z.junk, Z.sm, Z.xt, Z.xh, Z.ymix
            for si in range(len(subs)):
                loader, xsrc, xr, npart = subs[si]
                MMG([(B[hf][0:npart, :], ymix[:, kc, si * 128:si * 128 + npart], wout[:, kc, hf * 512:(hf + 1) * 512],
                      kc == 0, kc == 7) for hf in range(2) for kc in range(8)],
                    [wout.r] + ymix.res, [B[0].r, B[1].r])
                act(junk[0:npart, :], B[0][0:npart, :], AF.Square, [B[0].r], [junk.r, sm.r], accum=sm[0:npart, 8:9])
                act(junk[0:npart, :], B[1][0:npart, :], AF.Square, [B[1].r], [junk.r, sm.r], accum=sm[0:npart, 9:10])
                tt("dve", sm[0:npart, 10:11], sm[0:npart, 8:9], sm[0:npart, 9:10], ALU.add, [sm.r], [sm.r])
                rstd_from_ss(sm[0:npart, 10:11], sm[0:npart, 11:12], 1.0 / D, 1e-6, [sm.r], [sm.r])
                if loader is not None:
                    loader(xt)
                    xsrc, xr = xt[0:npart, :], xt.r
                gg = ggs if sample else ggp
                for hf in range(2):
                    cs = slice(hf * 512, (hf + 1) * 512)
                    stt(xh[0:npart, :], B[hf][0:npart, :], sm[0:npart, 11:12], gg[0:npart, cs], ALU.mult, ALU.mult,
                        [B[hf].r, sm.r, gg.r], [xh.r])
                    tt("dve", xsrc[:, cs], xh[0:npart, :], xsrc[:, cs], ALU.add, [xh.r, xr], [xr])
                store(si, npart, xsrc, xr)

        S.dma("sp", xs[:], xs_d[:, :], [], [xs.r])
        for l in range(DEPTH):
            areset()
            load_layer(l)
            prologue(l, l == 0)
            load_big(l)
            if stop <= 0:
                break
            areset()
            carve(TS, True)
            scl, cbs_f = Z.scl, Z.cbs_f
            for g4 in range(4):
                S.dma("sp", scl[:, g4, :], scv_d[l, g4 * 120:(g4 + 1) * 120, :], [], [scl.r])
            for cc in range(4):
                for g4 in range(4):
                    TR(B[5][:, g4 * 120:(g4 + 1) * 120], scl[0:120, g4, cc * 128:(cc + 1) * 128],
                       ident[0:120, 0:120], [scl.r, cst.r], [B[5].r])
                vcopy(cbs_f[:, cc, :, 0:30], B[5][:, 0:480].rearrange("p (s j) -> p s j", j=30),
                      [B[5].r], [cbs_f.res[cc]])
            xs_subs = [(None, xs[0:TS, :], xs.r, TS)]
            stage1(l, xs_subs, True, False)
            rwkv_tile(l, TS, True)
            conv_tile(l, TS, True)

            def store_s(si, npart, xsrc, xr, l=l):
                if l == DEPTH - 1:
                    S.dma("sp", ys_d[:, :], xs[0:TS, :], [xs.r], [])
            outproj(l, True, xs_subs, store_s)
            for cc in range(4):
                for g4 in range(4):
                    t8 = Z.tmp[7]
                    vcopy(t8[:, 0:120].rearrange("p (s j) -> p s j", j=30), cbs_f[:, cc, g4 * 4:(g4 + 1) * 4, 4:34],
                          [cbs_f.res[cc]], [t8.r])
                    TR(B[5][0:120, 0:128], t8[:, 0:120], ident, [t8.r, cst.r], [B[5].r])
                    vcopy(scl[0:120, g4, cc * 128:(cc + 1) * 128], B[5][0:120, 0:128], [B[5].r], [scl.r])
            for g4 in range(4):
                S.dma("sp", cvs_d[l, g4 * 120:(g4 + 1) * 120, :], scl[:, g4, :], [scl.r], [])
            if stop <= 1:
                break

            areset()
            carve(TT, False)
            Hf, Hb, carry, cbf, xh = Z.Hf, Z.Hb, Z.carry, Z.cbf, Z.xh
            G(lambda e, t_=Hf: e.memset(t_[:], 0.0), [], Hf.res)
            G(lambda e, t_=Hb: e.memset(t_[:], 0.0), [], Hb.res)
            G(lambda e, t_=carry: e.memset(t_[:], 0.0), [], [carry.r])
            for cc in range(4):
                G(lambda e, cc=cc, t_=cbf: e.memset(t_[:, cc, 0:30], 0.0), [], [cbf.res[cc]])
            src_d = xp_d if l == 0 else xmid_d
            dst_d = yp_d if l == DEPTH - 1 else xmid_d
            for ti in range(NT):
                subs = []
                for si in range(2):
                    r0 = ti * TT + si * 128

                    def loader(xt_, r0=r0, src_d=src_d):
                        S.dma("sp", xt_[:, :], src_d[r0:r0 + 128, :], [], [xt_.r])
                    subs.append((loader, None, None, 128))
                stage1(l, subs, False, ti == NT - 1)
                rwkv_tile(l, TT, False, conv_tasks(l, TT))
                conv_post(l, TT)

                def store_p(si, npart, xsrc, xr, ti=ti, dst_d=dst_d):
                    r0 = ti * TT + si * 128
                    S.dma("sp", dst_d[r0:r0 + 128, :], xsrc, [xr], [])
                outproj(l, False, subs, store_p)
                if stop <= 2:
                    break
            if stop <= 3:
                break
            for fc in range(4):
                TR(B[4][0:64, 0:128], Hf[:, fc, :], ident, [Hf.res[fc], cst.r], [B[4].r])
                vcopy(xh[0:64, fc * 128:(fc + 1) * 128], B[4][0:64, 0:128], [B[4].r], [xh.r])
            S.dma("sp", wkvp_d[l].rearrange("h v k -> v h k"), xh[0:64, :].rearrange("v (h k) -> v h k", k=64), [xh.r], [])
            for cc in range(4):
                TR(B[5][0:30, cc * 128:(cc + 1) * 128], cbf[:, cc, 0:30], ident, [cbf.res[cc], cst.r], [B[5].r])
            vcopy(xh[0:30, :], B[5][0:30, :], [B[5].r], [xh.r])
            S.dma("sp", cvp_d[l], xh[0:30, :], [xh.r], [])
        S.barrier()

        print("arena max use", amax)
        with nc.Block() as block:
            S.emit(block, None)
    return nc


_NC = None


def kernel(**inp):
    global _NC
    inp = {k: np.asarray(v) for k, v in inp.items()}
    if _NC is None:
        _NC = build()
    f32 = lambda a: np.ascontiguousarray(a, dtype=np.float32)
    prm = f32(np.stack([pack_params(inp, l) for l in range(DEPTH)]))
    prow = np.zeros((DEPTH, 17, 5 * D), np.float32)
    for l in range(DEPTH):
        prow[l, :, 0:3 * D] = inp["b_ada"][l][None]
        prow[l, :, 3 * D:4 * D] = inp["g_pre"][l][None]
        prow[l, :, 4 * D:5 * D] = inp["g_post"][l][None]
    lora = np.zeros((DEPTH, 128, 1024), np.float32)
    lora[:, 0:64, 0:512] = inp["w_up"]
    lora[:, 64:128, 512:1024] = inp["a_up"]
    in_maps = []
    for i in range(8):
        sl = slice(NSEQ * i, NSEQ * (i + 1))
        in_maps.append({
            "xp": f32(inp["x_prompt"][i]), "xs": f32(inp["x_sample"][sl].reshape(TS, D)),
            "cc": f32(np.concatenate([inp["c_prompt"][i:i + 1], inp["c_sample"][sl]], 0)),
            "ssh": f32(inp["state_shift"][:, sl]), "swkv": f32(inp["state_wkv"][:, sl]),
            "scv": f32(inp["state_conv"][:, sl].reshape(DEPTH, NSEQ * 30, 512)),
            "w_ada": f32(inp["w_ada"]), "w_in": f32(inp["w_in"]), "w_out": f32(inp["w_out"]),
            "prm": prm, "prow": prow, "lora": lora, "cst": CST, "cste": CSTE,
        })
    res = run_bass_kernel_spmd(_NC, in_maps, core_ids=list(range(8)))
    R = res.results
    cat = lambda k: np.stack([R[i][k] for i in range(8)])
    y_p = cat("yp")
    y_s = cat("ys").reshape(128, 4, D)
    sh_p = cat("shp").transpose(1, 0, 2)
    wkv_p = cat("wkvp").transpose(1, 0, 2, 3, 4)
    cv_p = cat("cvp").transpose(1, 0, 2, 3)
    sh_s = cat("shs").transpose(1, 0, 2, 3).reshape(DEPTH, 128, D)
    wkv_s = cat("wkvs").transpose(1, 0, 2, 3, 4, 5).reshape(DEPTH, 128, 8, 64, 64)
    cv_s = cat("cvs").reshape(8, DEPTH, NSEQ, 30, 512).transpose(1, 0, 2, 3, 4).reshape(DEPTH, 128, 30, 512)
    return tuple(np.ascontiguousarray(a, dtype=np.float32) for a in
                 (y_p, y_s, sh_p, wkv_p, cv_p, sh_s, wkv_s, cv_s))
```

```python
import math
import numpy as np
import concourse.bass as bass
import concourse.mybir as mybir
from concourse.bass_utils import run_bass_kernel_spmd

F32 = mybir.dt.float32
BF16 = mybir.dt.bfloat16
AF = mybir.ActivationFunctionType
ALU = mybir.AluOpType
AX = mybir.AxisListType

D = 1024
T = 2048
NSEQ = 16
TS = 64
DEPTH = 2
PIN = 3712
NCH = 29
TT = 256
NT = T // TT
CH = 64
C0 = math.exp(-0.5)
NPRM = 189


class Res:
    __slots__ = ("w", "rd", "excl")

    def __init__(self):
        self.w = None
        self.rd = {}
        self.excl = False


class Sched:
    ENG = ("pe", "dve", "act", "pool", "sp")

    def __init__(self, nc, ndma=40):
        self.nc = nc
        self.prog = {e: [] for e in self.ENG}
        self.cnt = {e: 0 for e in self.ENG}
        self.waited = {e: {} for e in self.ENG}
        self.sems = {}
        self.ndma = ndma
        self.dtot = [0] * ndma
        self.drange = {"sp": (0, ndma - 8), "pool": (ndma - 8, ndma)}
        self.dnext = {"sp": 0, "pool": ndma - 8}

    def alloc(self, stack):
        for e in self.ENG:
            self.sems[("e", e)] = stack.enter_context(self.nc.semaphore("s_" + e))
        for k in range(self.ndma):
            self.sems[("d", k)] = stack.enter_context(self.nc.semaphore("d%d" % k))

    def _need(self, eng, tok):
        if tok is None:
            return
        kind, key, val = tok
        if kind == "e" and key == eng and eng in ("pe", "sp"):
            return
        cur = self.waited[eng].get((kind, key), 0)
        if cur >= val:
            return
        self.waited[eng][(kind, key)] = val
        self.prog[eng].append(("w", (kind, key), val))

    def _deps(self, eng, reads, writes):
        for r in reads:
            self._need(eng, r.w)
            if r.excl:
                for (kind, key), val in r.rd.items():
                    if kind == "e" and key == eng:
                        continue
                    self._need(eng, (kind, key, val))
        for w in writes:
            self._need(eng, w.w)
            for (kind, key), val in w.rd.items():
                self._need(eng, (kind, key, val))

    def _commit(self, tok, reads, writes):
        k2 = (tok[0], tok[1])
        for r in reads:
            if r.rd.get(k2, 0) < tok[2]:
                r.rd[k2] = tok[2]
        for w in writes:
            w.w = tok
            w.rd = {}

    def op(self, eng, fn, reads=(), writes=()):
        self._deps(eng, reads, writes)
        self.cnt[eng] += 1
        tok = ("e", eng, self.cnt[eng])
        self.prog[eng].append(("o", fn))
        self._commit(tok, reads, writes)

    def dma(self, q, out, in_, reads=(), writes=()):
        k = self.dnext[q]
        lo, hi = self.drange[q]
        self.dnext[q] = lo + (k + 1 - lo) % (hi - lo)
        self._deps(q, reads, writes)
        self._need(q, ("d", k, self.dtot[k]) if self.dtot[k] else None)
        self.dtot[k] += 16
        tok = ("d", k, self.dtot[k])
        self.prog[q].append(("d", out, in_, k))
        self._commit(tok, reads, writes)

    def barrier(self):
        for e in self.ENG:
            for f in self.ENG:
                if f != e and self.cnt[f]:
                    self._need(e, ("e", f, self.cnt[f]))
            for k in range(self.ndma):
                if self.dtot[k]:
                    self._need(e, ("d", k, self.dtot[k]))

    def emit(self, block, final_wait):
        nc = self.nc
        hand = {"pe": block.tensor, "dve": block.vector, "act": block.scalar,
                "pool": block.gpsimd, "sp": block.sync}

        def mk(e):
            def body(h):
                for it in self.prog[e]:
                    if it[0] == "w":
                        h.wait_ge(self.sems[it[1]], it[2])
                    elif it[0] == "o":
                        it[1](h).then_inc(self.sems[("e", e)], 1)
                    else:
                        h.dma_start(out=it[1], in_=it[2]).then_inc(self.sems[("d", it[3])], 16)
                if e == "sp":
                    for k in range(self.ndma):
                        if self.dtot[k]:
                            h.wait_ge(self.sems[("d", k)], self.dtot[k])
            return body

        for e in self.ENG:
            hand[e](mk(e))


class Tl:
    def __init__(self, t, nres=1):
        self.t = t
        self.res = [Res() for _ in range(nres)]

    def __getitem__(self, k):
        return self.t[k]

    @property
    def r(self):
        return self.res[0]


def host_consts():
    p = np.arange(128)
    hp, tp = p // 64, p % 64
    same_h = hp[:, None] == hp[None, :]
    jlt = tp[:, None] < tp[None, :]
    jle = tp[:, None] <= tp[None, :]
    sq = (tp[:, None] // 4) == (tp[None, :] // 4)
    ident = np.eye(128)
    isel = (tp[:, None] == np.arange(64)[None, :]).astype(np.float64)
    bones = same_h.astype(np.float64)

    def masks(extra):
        su = (same_h & jlt & extra).astype(np.float64)
        u = (same_h & jle & extra).astype(np.float64)
        sl = su.T
        mA = np.concatenate([su, u], 1)
        mB = -mA
        mC = np.concatenate([-sl, np.ones((128, 64)), np.ones((128, 128)), -np.ones((128, 128))], 1)
        return mA, mB, mC
    mAp, mBp, mCp = masks(np.ones((128, 128), bool))
    mAs, mBs, mCs = masks(sq)
    rowsel = ((tp[:, None] // 4) == np.arange(16)[None, :]).astype(np.float64)
    rp = np.ones((128, TT)); rp[:, ::CH] = 0
    rs = np.ones((128, TS)); rs[:, ::4] = 0
    cols = [ident, isel, rowsel, rp, rs, bones, mAp, mBp, mCp, mAs, mBs, mCs]
    offs = {}
    names = ["ident", "isel", "rowsel", "rp", "rs", "bones", "mAp", "mBp", "mCp", "mAs", "mBs", "mCs"]
    o = 0
    for n, c in zip(names, cols):
        offs[n] = (o, c.shape[1])
        o += c.shape[1]
    cst = np.concatenate(cols, 1).astype(np.float32)
    E = np.zeros((17, 192), np.float32)
    E[0, 0:128] = 1.0
    for s in range(16):
        E[1 + s, 128 + 4 * s:128 + 4 * s + 4] = 1.0
    return cst, offs, E


CST, COFF, CSTE = host_consts()
NCST = CST.shape[1]
NCSTF = 128 + 64 + 16 + TT + TS


def pack_params(inp, l):
    fm = lambda v, n: np.ascontiguousarray(np.asarray(v, np.float32).reshape(n, 128).T)
    cols = [fm(inp["mu"][l], 17), fm(inp["w0"][l], 4), fm(inp["a0"][l], 4), fm(inp["k_k"][l], 4),
            fm(inp["k_a"][l], 4), fm(inp["r_k"][l].reshape(-1), 4), fm(inp["gn_r_g"][l], 4),
            fm(inp["gn_r_b"][l], 4), fm(inp["b_dw"][l], 4), fm(inp["gn_c_g"][l], 4),
            fm(inp["gn_c_b"][l], 4)]
    wd = np.asarray(inp["w_dw"][l], np.float32)
    wdT = wd.T.reshape(4, 128, 31).transpose(1, 0, 2).reshape(128, 124)
    cols.append(wdT)
    cols.append(fm(inp["g_pre"][l], 8))
    return np.concatenate(cols, 1).astype(np.float32)


PO = {"mu": 0, "w0": 17, "a0": 21, "k_k": 25, "k_a": 29, "r_k": 33, "gn_r_g": 37, "gn_r_b": 41,
      "b_dw": 45, "gn_c_g": 49, "gn_c_b": 53, "wdw": 57, "g_pre": 181}


def build(dbg=None, stop=99, stop2=10**9):
    dbg = None
    from contextlib import ExitStack
    nc = bass.Bass("TRN2", target_bir_lowering=False)
    di = lambda n, s: nc.dram_tensor(n, list(s), F32, kind="ExternalInput").ap()
    do = lambda n, s: nc.dram_tensor(n, list(s), F32, kind="ExternalOutput").ap()
    xp_d = di("xp", (T, D)); xs_d = di("xs", (TS, D)); cc_d = di("cc", (17, D))
    ssh_d = di("ssh", (DEPTH, NSEQ, D)); swkv_d = di("swkv", (DEPTH, NSEQ, 8, 64, 64))
    scv_d = di("scv", (DEPTH, NSEQ * 30, 512))
    wada_d = di("w_ada", (DEPTH, D, 3 * D)); win_d = di("w_in", (DEPTH, D, PIN))
    wout_d = di("w_out", (DEPTH, D, D))
    prm_d = di("prm", (DEPTH, 128, NPRM)); prow_d = di("prow", (DEPTH, 17, 5 * D))
    lora_d = di("lora", (DEPTH, 128, 1024)); cst_d = di("cst", (128, NCST)); cste_d = di("cste", (17, 192))
    yp_d = do("yp", (T, D)); ys_d = do("ys", (TS, D)); shp_d = do("shp", (DEPTH, D))
    wkvp_d = do("wkvp", (DEPTH, 8, 64, 64)); cvp_d = do("cvp", (DEPTH, 30, 512))
    shs_d = do("shs", (DEPTH, NSEQ, D)); wkvs_d = do("wkvs", (DEPTH, NSEQ, 8, 64, 64))
    cvs_d = do("cvs", (DEPTH, NSEQ * 30, 512))
    xmid_d = yp_d
    dbg_d = do("dbg", (128, 8192)) if dbg else None

    with ExitStack() as st:
        S = Sched(nc)
        S.alloc(st)

        def sb(name, shape, dt=F32, nres=1, stack=st):
            return Tl(stack.enter_context(nc.sbuf_tensor("sb_" + name, list(shape), dt)), nres)

        def ps(name, shape, dt=F32):
            return Tl(st.enter_context(nc.psum_tensor(name, list(shape), dt)))

        V = lambda fn, r=(), w=(): S.op("dve", fn, r, w)
        A = lambda fn, r=(), w=(): S.op("act", fn, r, w)
        G = lambda fn, r=(), w=(): S.op("pool", fn, r, w)
        P = lambda fn, r=(), w=(): S.op("pe", fn, r, w)

        def MM(out, lhsT, rhs, r, w, start=True, stop=True):
            P(lambda e: e.matmul(out, lhsT, rhs, start=start, stop=stop), r, w)

        def MMG(lst, r, w):
            def fn(e):
                ins = None
                for (o, l, rh, s0, s1) in lst:
                    ins = e.matmul(o, l, rh, start=s0, stop=s1)
                return ins
            P(fn, r, w)

        def TR(out, in_, idn, r, w):
            P(lambda e: e.transpose(out, in_, idn), r, w)

        def act(out, in_, func, r, w, scale=1.0, bias=0.0, accum=None):
            if accum is None:
                A(lambda e: e.activation(out=out, in_=in_, func=func, bias=bias, scale=scale), r, w)
            else:
                A(lambda e: e.activation(out=out, in_=in_, func=func, bias=bias, scale=scale,
                                         accum_out=accum), r, w)

        def tt(eng, out, a, b, op, r, w):
            S.op(eng, lambda e: e.tensor_tensor(out, a, b, op), r, w)

        def ts(eng, out, a, s1, s2, op0, op1, r, w):
            if op1 is None:
                S.op(eng, lambda e: e.tensor_scalar(out, a, s1, None, op0), r, w)
            else:
                S.op(eng, lambda e: e.tensor_scalar(out, a, s1, s2, op0, op1), r, w)

        def stt(out, a, sc, b, op0, op1, r, w):
            V(lambda e: e.scalar_tensor_tensor(out, a, sc, b, op0, op1), r, w)

        dpos = [0]

        def dump(name, ap, res, n):
            if dbg_d is None:
                return
            o = dpos[0]
            dpos[0] += n
            print("DUMP", name, o, n)
            S.dma("sp", dbg_d[0:128, o:o + n], ap, [res], [])

        def sigmoid_from_exp(t, rr):
            ts("dve", t, t, 1.0, None, ALU.add, None, [rr], [rr])
            V(lambda e: e.reciprocal(t, t), [rr], [rr])

        cst = sb("cst", (128, NCSTF))
        cstb = sb("cstb", (128, NCST), BF16)
        cste = sb("cste", (17, 192))
        S.dma("sp", cst[:], cst_d[:, 0:NCSTF], [], [cst.r])
        S.dma("pool", cstb[:], cst_d[:, :], [], [cstb.r])
        S.dma("sp", cste[:], cste_d[:, :], [], [cste.r])
        cf = lambda n: cst[:, COFF[n][0]:COFF[n][0] + COFF[n][1]]
        cb = lambda n: cstb[:, COFF[n][0]:COFF[n][0] + COFF[n][1]]

        win = sb("win", (128, 8, PIN), BF16, nres=1)
        wout = sb("wout", (128, 8, D), BF16)
        prm = sb("prm", (128, NPRM))
        drv = sb("drv", (128, 64))
        lora = sb("lora", (128, 1024), BF16)
        gsT = sb("gsT", (128, 8, 17)); shT = sb("shT", (128, 8, 17))
        ggp = sb("ggp", (128, D)); ggs = sb("ggs", (TS, D))
        gss = sb("gss", (TS, D)); shs_t = sb("shs_t", (TS, D))
        cT = sb("cT", (128, 8, 17), BF16)
        xs = sb("xs", (TS, D))
        hlast = sb("hlast", (128, 8))
        PBD = []
        for q_ in range(2):
            d_ = {"KRbd": sb("KRbd%d" % q_, (128, 4, 2, 128), BF16), "bbd": sb("bbd%d" % q_, (128, 4, 128), BF16),
                  "kbd": sb("kbd%d" % q_, (128, 4, 128), BF16), "vbd": sb("vbd%d" % q_, (128, 4, 128), BF16),
                  "ynbd": sb("ynbd%d" % q_, (128, 128), BF16)}
            for tl in d_.values():
                G(lambda e, tl=tl: e.memset(tl[:], 0.0), [], [tl.r])
            PBD.append(d_)

        B = [ps("bk%d" % i, (128, 512)) for i in range(8)]
        for b_ in B:
            b_.r.excl = True

        rem_ = nc.sbuf_bytes_remaining
        NB = 13400
        NF = (rem_ - NB * 2) // 4 // 4 * 4 - 8
        print("SBUF remaining", rem_, "NF", NF, "NB", NB)
        arF = st.enter_context(nc.sbuf_tensor("arF", [128, NF], F32))
        arB = st.enter_context(nc.sbuf_tensor("arB", [128, NB], BF16))
        apos = {"f": 0, "b": 0}

        def areset():
            S.barrier()
            apos["f"] = 0
            apos["b"] = 0

        def ar(shape, dt=F32, nres=1):
            n = 1
            for d_ in shape[1:]:
                n *= d_
            key = "f" if dt == F32 else "b"
            base = arF if dt == F32 else arB
            o = apos[key]
            n2 = (n + 3) // 4 * 4
            apos[key] = o + n2
            assert apos[key] <= (NF if dt == F32 else NB), (key, apos[key])
            ap = base[0:shape[0], o:o + n]
            if len(shape) == 3:
                ap = ap.rearrange("p (a b) -> p a b", b=shape[2])
            elif len(shape) == 4:
                ap = ap.rearrange("p (a b c) -> p a b c", b=shape[2], c=shape[3])
            amax[key] = max(amax[key], apos[key])
            return Tl(ap, nres)

        amax = {"f": 0, "b": 0}

        class NS:
            pass
        Z = NS()

        def make_set(q_, W):
            bs = NS()
            bs.q = q_
            bs.bv = ar((128, W)); bs.sgr = ar((128, W)); bs.dec = ar((128, 16))
            bs.M1 = [ar((128, 256), BF16) for i in range(2)]
            bs.M2 = [ar((128, 256), BF16) for i in range(2)]
            bs.C4 = [ar((128, 448), BF16) for i in range(2)]
            bs.TmT = [ar((128, 128), BF16) for i in range(2)]
            bs.BmV = [ar((128, 64)) for i in range(2)]
            bs.XT = [ar((128, 3, 128), BF16) for i in range(2)]
            bs.Xb = ar((128, 64), BF16); bs.Ub = ar((128, 64), BF16)
            bs.y1 = ar((128, 64)); bs.yv = ar((128, 64)); bs.t64 = ar((128, 64)); bs.t64b = ar((128, 64))
            bs.st6 = ar((128, 6)); bs.mv = ar((128, 2)); bs.rs1 = ar((128, 2))
            for k_, v_ in PBD[q_].items():
                setattr(bs, k_, v_)
            bs.bk = [B[4], B[5], B[6], B[7]] if q_ == 0 else [B[0], B[1], B[2], B[3]]
            return bs

        def carve(W, sample):
            WP = W + (NSEQ if sample else 0)
            Z.hT = ar((128, 8, WP), BF16)
            Z.xt2 = [ar((128, D)) for i in range(1 if sample else 2)]
            Z.xt = Z.xt2[0]
            Z.xh = ar((128, 512))
            Z.o30 = Z.xh
            Z.junk = ar((128, 512), BF16)
            Z.sm = ar((128, 16))
            Z.Uc = [ar((128, WP + 1)) for i in range(2)]
            Z.Up = ar((128, TS)) if sample else None
            Z.urw = ar((128, 17, W), nres=17)
            Z.sgA = ar((128, 4, W), nres=4)
            Z.aA = ar((128, 4, W), nres=4)
            Z.lob = ar((128, W), BF16)
            Z.tmp = [ar((128, max(W, 128))) for i in range(8)]
            Z.tb = [ar((128, W), BF16) for i in range(2)]
            Z.dg = ar((128, 16, 128), BF16) if sample else None
            Z.ymix = ar((128, 8, W), BF16, nres=8)
            Z.sets = [make_set(0, W)] + ([] if sample else [make_set(1, W)])
            if sample:
                Z.Hsf = ar((128, 16, 64)); Z.Hsb = ar((128, 16, 64), BF16)
                Z.sw_in = ar((64, NSEQ, 2, 64), nres=NSEQ); Z.sw_out = ar((64, NSEQ, 2, 64), nres=NSEQ)
                Z.Gs = ar((128, 16, 64))
                Z.vsel = ar((128, 16, 64), BF16); Z.usel = ar((128, 16, 64), BF16)
                Z.cbs_f = ar((128, 4, 16, 34), nres=4)
                Z.cbs_b = ar((128, 4, 16, 34), BF16, nres=4)
                Z.scl = ar((120, 4, 512))
            else:
                Z.Hf = ar((128, 4, 64), nres=4); Z.Hb = ar((128, 4, 64), BF16, nres=4)
                Z.carry = ar((128, 17))
                Z.cbf = ar((128, 4, 30 + W), nres=4)
                Z.cacc = ar((128, 4, W), nres=4)
                Z.caccP = ar((128, W)); Z.ctmpP2 = [ar((128, W)) for i in range(2)]

        ident = cf("ident"); identb = cb("ident")
        bonesb = cb("bones"); iself = cf("isel"); iselb = cb("isel")

        def load_big(l):
            for kc in range(8):
                S.dma("pool", win[:, kc, :], win_d[l, kc * 128:(kc + 1) * 128, :], [], [win.r])
            for kc in range(8):
                S.dma("pool", wout[:, kc, :], wout_d[l, kc * 128:(kc + 1) * 128, :], [], [wout.r])

        def load_layer(l):
            S.dma("sp", prm[:], prm_d[l], [], [prm.r])
            S.dma("pool", lora[:], lora_d[l], [], [lora.r])
            ts("dve", drv[:, 0:17], prm[:, 0:17], -1.0, 1.0, ALU.mult, ALU.add, [prm.r], [drv.r])
            ts("dve", drv[:, 17:25], prm[:, 17:25], -1.0, None, ALU.mult, None, [prm.r], [drv.r])
            ts("dve", drv[:, 25:29], prm[:, 29:33], -1.0, 1.0, ALU.mult, ALU.add, [prm.r], [drv.r])
            ts("dve", drv[:, 29:37], prm[:, 49:57], -1.0, None, ALU.mult, None, [prm.r], [drv.r])

        pcol = lambda name, i=0: prm[:, PO[name] + i:PO[name] + i + 1]
        dcol = lambda o: drv[:, o:o + 1]

        def prologue(l, first):
            modr = ar((17, 3 * D)); gsr = ar((17, D)); ggr = ar((17, D))
            wab = [ar((128, 3 * D), BF16) for i in range(2)]
            if first:
                ccx, cce = gsr, ggr
                S.dma("sp", ccx[:], cc_d[:, :], [], [ccx.r])
                act(cce[:], ccx[:], AF.Exp, [ccx.r], [cce.r], scale=-1.0)
                sigmoid_from_exp(cce[:], cce.r)
                tt("dve", cce[:], cce[:], ccx[:], ALU.mult, [ccx.r, cce.r], [cce.r])
                for kc in range(8):
                    TR(B[0][:, kc * 17:(kc + 1) * 17], cce[0:17, kc * 128:(kc + 1) * 128],
                       ident[0:17, 0:17], [cce.r, cst.r], [B[0].r])
                V(lambda e: e.tensor_copy(cT[:].rearrange("p a b -> p (a b)"), B[0][:, 0:136]),
                  [B[0].r], [cT.r])
            S.dma("sp", modr[:], prow_d[l, :, 0:3 * D], [], [modr.r])
            S.dma("sp", gsr[:], prow_d[l, :, 3 * D:4 * D], [], [gsr.r])
            S.dma("sp", ggr[:], prow_d[l, :, 4 * D:5 * D], [], [ggr.r])
            for kc in range(8):
                wb = wab[kc % 2]
                S.dma("pool", wb[:], wada_d[l, kc * 128:(kc + 1) * 128, :], [], [wb.r])
                for nb in range(6):
                    MM(B[nb][0:17, :], cT[:, kc, :], wb[:, nb * 512:(nb + 1) * 512],
                       [cT.r, wb.r], [B[nb].r], start=(kc == 0), stop=(kc == 7))
            for nb in range(6):
                cs = slice(nb * 512, (nb + 1) * 512)
                tt("dve", modr[:, cs], B[nb][0:17, :], modr[:, cs], ALU.add, [B[nb].r, modr.r], [modr.r])
            stt(gsr[:], modr[:, D:2 * D], 1.0, gsr[:], ALU.add, ALU.mult, [modr.r, gsr.r], [gsr.r])
            tt("dve", ggr[:], modr[:, 2 * D:3 * D], ggr[:], ALU.mult, [modr.r, ggr.r], [ggr.r])
            for kc in range(8):
                TR(B[0][:, kc * 17:(kc + 1) * 17], gsr[0:17, kc * 128:(kc + 1) * 128],
                   ident[0:17, 0:17], [gsr.r, cst.r], [B[0].r])
                TR(B[1][:, kc * 17:(kc + 1) * 17], modr[0:17, kc * 128:(kc + 1) * 128],
                   ident[0:17, 0:17], [modr.r, cst.r], [B[1].r])
            V(lambda e: e.tensor_copy(gsT[:].rearrange("p a b -> p (a b)"), B[0][:, 0:136]), [B[0].r], [gsT.r])
            V(lambda e: e.tensor_copy(shT[:].rearrange("p a b -> p (a b)"), B[1][:, 0:136]), [B[1].r], [shT.r])
            for hf in range(2):
                cs = slice(hf * 512, (hf + 1) * 512)
                MM(B[2][:, :], cste[:, 0:128], ggr[:, cs], [cste.r, ggr.r], [B[2].r])
                V(lambda e, cs=cs: e.tensor_copy(ggp[:, cs], B[2][:, :]), [B[2].r], [ggp.r])
                MM(B[3][0:TS, :], cste[:, 128:192], ggr[:, cs], [cste.r, ggr.r], [B[3].r])
                V(lambda e, cs=cs: e.tensor_copy(ggs[:, cs], B[3][0:TS, :]), [B[3].r], [ggs.r])
                MM(B[4][0:TS, :], cste[:, 128:192], gsr[:, cs], [cste.r, gsr.r], [B[4].r])
                V(lambda e, cs=cs: e.tensor_copy(gss[:, cs], B[4][0:TS, :]), [B[4].r], [gss.r])
                MM(B[5][0:TS, :], cste[:, 128:192], modr[:, cs], [cste.r, modr.r], [B[5].r])
                V(lambda e, cs=cs: e.tensor_copy(shs_t[:, cs], B[5][0:TS, :]), [B[5].r], [shs_t.r])

        def rstd_from_ss(ss_ap, out_ap, scale, eps, r, w):
            act(out_ap, ss_ap, AF.Ln, r, w, scale=scale, bias=eps)
            act(out_ap, out_ap, AF.Exp, w, w, scale=-0.5)

        v3 = lambda ap: ap.rearrange("p (s t) -> p s t", t=4)

        def vcopy(out, in_, r, w):
            V(lambda e: e.tensor_copy(out, in_), r, w)

        def gcopy(out, in_, r, w):
            G(lambda e: e.tensor_copy(out, in_), r, w)

        def stage1(l, subs, sample, want_last):
            hT, junk, sm, tmp, xh = Z.hT, Z.junk, Z.sm, Z.tmp, Z.xh
            nst = len(subs)
            W = nst * 128 if not sample else TS
            for si in range(nst):
                xt = Z.xt2[si % len(Z.xt2)]
                loader, xsrc, xres_r, npart = subs[si]
                if loader is not None:
                    loader(xt)
                    xsrc, xres_r = xt[0:npart, :], xt.r
                for hf in range(2):
                    cs = slice(hf * 512, (hf + 1) * 512)
                    act(junk[0:npart, :], xsrc[:, cs], AF.Square, [xres_r], [junk.r, sm.r],
                        accum=sm[0:npart, 4 * si + hf:4 * si + hf + 1])
                tt("dve", sm[0:npart, 4 * si + 2:4 * si + 3], sm[0:npart, 4 * si:4 * si + 1],
                   sm[0:npart, 4 * si + 1:4 * si + 2], ALU.add, [sm.r], [sm.r])
                rstd_from_ss(sm[0:npart, 4 * si + 2:4 * si + 3], sm[0:npart, 4 * si + 3:4 * si + 4], 1.0 / D, 1e-6,
                             [sm.r], [sm.r])
                ts("dve", xt[0:npart, :], xsrc, sm[0:npart, 4 * si + 3:4 * si + 4], None, ALU.mult, None,
                   [xres_r, sm.r], [xt.r])
                if sample:
                    for hf in range(2):
                        cs = slice(hf * 512, (hf + 1) * 512)
                        tt("dve", xh[0:TS, :], xt[0:TS, cs], gss[:, cs], ALU.mult, [xt.r, gss.r], [xh.r])
                        tt("dve", xh[0:TS, :], xh[0:TS, :], shs_t[:, cs], ALU.add, [xh.r, shs_t.r], [xh.r])
                        for s_ in range(NSEQ):
                            S.dma("sp", shs_d[l, s_:s_ + 1, cs], xh[4 * s_ + 3:4 * s_ + 4, :], [xh.r], [])
                for kc in range(8):
                    bank = B[kc // 2]
                    col = (kc % 2) * 256 + si * 128
                    TR(bank[:, col:col + npart], xt[0:npart, kc * 128:(kc + 1) * 128],
                       ident[0:npart, 0:npart], [xt.r, cst.r], [bank.r])
            if sample:
                xt = Z.xt2[0]
                S.dma("sp", xt[0:NSEQ, :], ssh_d[l], [], [xt.r])
                for kc in range(8):
                    bank = B[kc // 2]
                    col = (kc % 2) * 256 + TS
                    TR(bank[:, col:col + NSEQ], xt[0:NSEQ, kc * 128:(kc + 1) * 128],
                       ident[0:NSEQ, 0:NSEQ], [xt.r, cst.r], [bank.r])
            for kc in range(8):
                bank = B[kc // 2]
                c0 = (kc % 2) * 256
                if not sample:
                    act(hT[:, kc, 0:W], bank[:, c0:c0 + W], AF.Identity, [bank.r, gsT.r, shT.r], [hT.r],
                        scale=gsT[:, kc, 0:1], bias=shT[:, kc, 0:1])
                    if want_last:
                        act(hlast[:, kc:kc + 1], bank[:, c0 + W - 1:c0 + W], AF.Identity,
                            [bank.r, gsT.r, shT.r], [hlast.r], scale=gsT[:, kc, 0:1], bias=shT[:, kc, 0:1])
                else:
                    tt("dve", v3(tmp[0][:, 0:TS]), v3(bank[:, c0:c0 + TS]),
                       gsT[:, kc, 1:17].unsqueeze(2).to_broadcast([128, 16, 4]), ALU.mult,
                       [bank.r, gsT.r], [tmp[0].r])
                    tt("dve", v3(hT[:, kc, 0:TS]), v3(tmp[0][:, 0:TS]),
                       shT[:, kc, 1:17].unsqueeze(2).to_broadcast([128, 16, 4]), ALU.add,
                       [tmp[0].r, shT.r], [hT.r])
                    vcopy(hT[:, kc, TS:TS + NSEQ], bank[:, c0 + TS:c0 + TS + NSEQ], [bank.r], [hT.r])
            if want_last:
                TR(B[2][0:8, 0:128], hlast[:, 0:8], ident, [hlast.r, cst.r], [B[2].r])
                vcopy(xh[0:8, 0:128], B[2][0:8, 0:128], [B[2].r], [xh.r])
                S.dma("sp", shp_d[l].rearrange("(c p) -> c p", p=128), xh[0:8, 0:128], [xh.r], [])

        def project(c, W, bank):
            hT = Z.hT
            MMG([(bank[:, 0:W], win[:, kc, c * 128:(c + 1) * 128], hT[:, kc, 0:W], kc == 0, kc == 7)
                 for kc in range(8)], [win.r, hT.r], [bank.r])

        def run_tasks(tasks):
            done = set()
            pending = list(tasks)
            active = []
            while pending or active:
                still = []
                for t_ in pending:
                    if all(d_ in done for d_ in t_[2]):
                        active.append((t_[0], t_[1]()))
                    else:
                        still.append(t_)
                pending = still
                assert active, "task deadlock"
                nxt = []
                for (n_, g_) in active:
                    try:
                        next(g_)
                        nxt.append((n_, g_))
                    except StopIteration:
                        done.add(n_)
                active = nxt

        def rwkv_tile(l, W, sample, extra_tasks=()):
            urw, sgA, aA, lob, tmp, tb, ymix, Uc = Z.urw, Z.sgA, Z.aA, Z.lob, Z.tmp, Z.tb, Z.ymix, Z.Uc
            WP = W + (NSEQ if sample else 0)
            mA, mB, mC = (cb("mAs"), cb("mBs"), cb("mCs")) if sample else (cb("mAp"), cb("mBp"), cb("mCp"))
            rmask = cf("rs") if sample else cf("rp")
            nchunk = W // CH
            nlev = 6 if not sample else 2
            order = [16] + list(range(16))
            for i, c in enumerate(order):
                bank = B[2 + (i % 2)]
                u = Uc[i % 2]
                project(c, WP, bank)
                if not sample:
                    carry = Z.carry
                    gcopy(u[:, 0:1], carry[:, c:c + 1], [carry.r], [u.r])
                    act(u[:, 1:1 + W], bank[:, 0:W], AF.Copy, [bank.r], [u.r])
                    gcopy(carry[:, c:c + 1], u[:, W:W + 1], [u.r], [carry.r])
                    tt("pool", tmp[0][:, 0:W], u[:, 0:W], u[:, 1:1 + W], ALU.subtract, [u.r], [tmp[0].r])
                    stt(urw[:, c, 0:W], tmp[0][:, 0:W], pcol("mu", c), u[:, 1:1 + W], ALU.mult, ALU.add,
                        [tmp[0].r, u.r, prm.r], [urw.res[c]])
                else:
                    Up = Z.Up
                    act(u[:, 0:WP], bank[:, 0:WP], AF.Copy, [bank.r], [u.r])
                    vcopy(v3(Up[:, 0:TS])[:, :, 0:1], u[:, TS:TS + NSEQ].unsqueeze(2), [u.r], [Up.r])
                    vcopy(v3(Up[:, 0:TS])[:, :, 1:4], v3(u[:, 0:TS])[:, :, 0:3], [u.r], [Up.r])
                    tt("dve", tmp[0][:, 0:W], Up[:, 0:W], u[:, 0:W], ALU.subtract, [u.r, Up.r], [tmp[0].r])
                    stt(urw[:, c, 0:W], tmp[0][:, 0:W], pcol("mu", c), u[:, 0:W], ALU.mult, ALU.add,
                        [tmp[0].r, u.r, prm.r], [urw.res[c]])
            e2 = tmp[1]
            act(e2[0:64, 0:W], urw[0:64, 16, 0:W], AF.Exp, [urw.res[16]], [e2.r], scale=2.0)
            ts("dve", e2[0:64, 0:W], e2[0:64, 0:W], 1.0, None, ALU.add, None, [e2.r], [e2.r])
            V(lambda e: e.reciprocal(e2[0:64, 0:W], e2[0:64, 0:W]), [e2.r], [e2.r])
            ts("dve", lob[0:64, 0:W], e2[0:64, 0:W], -2.0, 1.0, ALU.mult, ALU.add, [e2.r], [lob.r])
            vcopy(lob[64:128, 0:W], urw[64:128, 16, 0:W], [urw.res[16]], [lob.r])
            for fc in range(4):
                MM(B[4][:, 0:W], lora[:, fc * 128:(fc + 1) * 128], lob[:, 0:W], [lora.r, lob.r], [B[4].r])
                MM(B[5][:, 0:W], lora[:, 512 + fc * 128:512 + (fc + 1) * 128], lob[:, 0:W], [lora.r, lob.r], [B[5].r])
                act(sgA[:, fc, 0:W], B[4][:, 0:W], AF.Exp, [B[4].r, drv.r], [sgA.res[fc]],
                    scale=-1.0, bias=dcol(17 + fc))
                act(aA[:, fc, 0:W], B[5][:, 0:W], AF.Exp, [B[5].r, drv.r], [aA.res[fc]],
                    scale=-1.0, bias=dcol(21 + fc))
                sigmoid_from_exp(sgA[:, fc, 0:W], sgA.res[fc])
                sigmoid_from_exp(aA[:, fc, 0:W], aA.res[fc])

            Lsg, Lx, eP, eN, ePx, kk, kmod, beta = [tmp[i] for i in range(8)]
            t1, ssc = Lsg, Lx
            w_ = lambda t: t[:, 0:W]
            c3 = lambda ap: ap.rearrange("p (c t) -> p c t", t=CH)

            def t_prep(fc, q):
                r_ = urw[:, fc, 0:W]; k_ = urw[:, 4 + fc, 0:W]; v_ = urw[:, 8 + fc, 0:W]; g_ = urw[:, 12 + fc, 0:W]
                rr, rk, rv, rg = urw.res[fc], urw.res[4 + fc], urw.res[8 + fc], urw.res[12 + fc]
                sg = sgA[:, fc, 0:W]; a_ = aA[:, fc, 0:W]
                bk0, bk1 = q.bk[0], q.bk[1]
                if sample:
                    Hsf, Hsb, sw_in = Z.Hsf, Z.Hsb, Z.sw_in
                    for s_ in range(NSEQ):
                        S.dma("sp", sw_in[:, s_, :, :], swkv_d[l, s_, 2 * fc:2 * fc + 2].rearrange("h v k -> v h k"),
                              [], [sw_in.res[s_]])
                    for s8 in range(0, NSEQ, 8):
                        for s_ in range(s8, s8 + 8):
                            TR(bk0[:, (s_ - s8) * 64:(s_ - s8 + 1) * 64],
                               sw_in[:, s_, :, :].rearrange("v h k -> v (h k)"),
                               ident[0:64, 0:64], [sw_in.res[s_], cst.r], [bk0.r])
                        vcopy(Hsf[:, s8:s8 + 8, :].rearrange("p s v -> p (s v)"), bk0[:, :], [bk0.r], [Hsf.r])
                        vcopy(Hsb[:, s8:s8 + 8, :].rearrange("p s v -> p (s v)"), bk0[:, :], [bk0.r], [Hsb.r])
                    yield
                V(lambda e, o_=w_(Lsg), m_=rmask[:, 0:W], s_in=sg: e.tensor_tensor_scan(
                    o_, m_, s_in, 0.0, ALU.mult, ALU.add), [sgA.res[fc], cst.r], [Lsg.r])
                tt("pool", w_(Lx), w_(Lsg), sg, ALU.subtract, [Lsg.r, sgA.res[fc]], [Lx.r])
                yield
                act(w_(eP), w_(Lsg), AF.Exp, [Lsg.r], [eP.r], scale=-C0)
                act(w_(eN), w_(Lsg), AF.Exp, [Lsg.r], [eN.r], scale=C0)
                act(w_(ePx), w_(Lx), AF.Exp, [Lx.r], [ePx.r], scale=-C0)
                yield
                ts("dve", w_(kk), k_, pcol("k_k", fc), None, ALU.mult, None, [rk, prm.r], [kk.r])
                act(tb[0][:, 0:W], w_(kk), AF.Square, [kk.r], [tb[0].r])
                MM(bk0[:, 0:W], bonesb, tb[0][:, 0:W], [cstb.r, tb[0].r], [bk0.r])
                yield
                ts("dve", w_(ssc), bk0[:, 0:W], 1e-24, None, ALU.max, None, [bk0.r], [ssc.r])
                act(w_(ssc), w_(ssc), AF.Ln, [ssc.r], [ssc.r])
                act(w_(ssc), w_(ssc), AF.Exp, [ssc.r], [ssc.r], scale=-0.5)
                tt("dve", w_(kk), w_(kk), w_(ssc), ALU.mult, [kk.r, ssc.r], [kk.r])
                yield
                ts("dve", w_(t1), a_, pcol("k_a", fc), dcol(25 + fc), ALU.mult, ALU.add,
                   [aA.res[fc], prm.r, drv.r], [t1.r])
                tt("pool", w_(kmod), k_, w_(t1), ALU.mult, [rk, t1.r], [kmod.r])
                tt("pool", w_(beta), w_(kk), a_, ALU.mult, [kk.r, aA.res[fc]], [beta.r])
                yield
                stt(tb[1][:, 0:W], r_, pcol("r_k", fc), w_(kmod), ALU.mult, ALU.mult,
                    [rr, prm.r, kmod.r], [tb[1].r])
                MM(bk1[:, 0:W], bonesb, tb[1][:, 0:W], [cstb.r, tb[1].r], [bk1.r])
                tt("dve", q.bv[:, 0:W], bk1[:, 0:W], v_, ALU.mult, [bk1.r, rv], [q.bv.r])
                yield
                act(q.sgr[:, 0:W], g_, AF.Exp, [rg], [q.sgr.r], scale=-1.0)
                sigmoid_from_exp(q.sgr[:, 0:W], q.sgr.r)
                tt("dve", q.sgr[:, 0:W], q.sgr[:, 0:W], g_, ALU.mult, [q.sgr.r, rg], [q.sgr.r])
                yield
                if not sample:
                    vcopy(q.dec[:, 0:nchunk], eP[:, CH - 1:W:CH], [eP.r], [q.dec.r])
                else:
                    vcopy(q.dec[:, 0:16], eP[:, 3:W:4], [eP.r], [q.dec.r])
                for hh in range(2):
                    pp = slice(hh * 64, (hh + 1) * 64)
                    tt("pool", q.KRbd[pp, 0:nchunk, 0, pp], c3(kk[pp, 0:W]), c3(ePx[pp, 0:W]), ALU.mult,
                       [kk.r, ePx.r], [q.KRbd.r])
                    tt("dve", q.KRbd[pp, 0:nchunk, 1, pp], c3(urw[pp, fc, 0:W]), c3(eP[pp, 0:W]), ALU.mult,
                       [rr, eP.r], [q.KRbd.r])
                    tt("pool", q.bbd[pp, 0:nchunk, pp], c3(beta[pp, 0:W]), c3(eN[pp, 0:W]), ALU.mult,
                       [beta.r, eN.r], [q.bbd.r])
                    tt("dve", q.kbd[pp, 0:nchunk, pp], c3(kmod[pp, 0:W]), c3(eN[pp, 0:W]), ALU.mult,
                       [kmod.r, eN.r], [q.kbd.r])
                    gcopy(q.vbd[pp, 0:nchunk, pp], c3(urw[pp, 8 + fc, 0:W]), [rv], [q.vbd.r])
                    yield

            def t_pre(fc, ch, q):
                b_ = ch % 2
                M1, M2, C4, TmT, BmV, XT = q.M1[b_], q.M2[b_], q.C4[b_], q.TmT[b_], q.BmV[b_], q.XT
                bkp, bkc, bkn, bkx = q.bk
                KR = q.KRbd[:, ch, :, :].rearrange("p a b -> p (a b)")
                kap = q.KRbd[:, ch, 0, :]
                MM(bkp[:, 0:256], q.kbd[:, ch, :], KR, [q.kbd.r, q.KRbd.r], [bkp.r])
                MM(bkp[:, 256:512], q.bbd[:, ch, :], KR, [q.bbd.r, q.KRbd.r], [bkp.r])
                MMG([(bkc[:, 0:128], kap, q.bbd[:, ch, :], True, True),
                     (bkc[:, 128:192], q.vbd[:, ch, :], iselb, True, True),
                     (bkc[:, 192:320], q.kbd[:, ch, :], identb, True, True),
                     (bkc[:, 320:448], q.bbd[:, ch, :], identb, True, True)],
                    [q.KRbd.r, q.bbd.r, q.vbd.r, q.kbd.r, cstb.r], [bkc.r])
                yield
                tt("dve", M1[:], bkp[:, 0:256], mA, ALU.mult, [bkp.r, cstb.r], [M1.r])
                tt("dve", M2[:], bkp[:, 256:512], mB, ALU.mult, [bkp.r, cstb.r], [M2.r])
                tt("dve", C4[:], bkc[:, 0:448], mC, ALU.mult, [bkc.r, cstb.r], [C4.r])
                yield
                BmT = M1[:, 0:128]
                Q0 = M2[:, 0:128]
                N0 = C4[:, 0:128]; Vtok = C4[:, 128:192]
                xa = XT[0]
                gcopy(xa[:, 0, :], Q0, [M2.r], [xa.r])
                tt("pool", xa[:, 1, :], Q0, identb, ALU.add, [M2.r, cstb.r], [xa.r])
                gcopy(xa[:, 2, :], N0, [C4.r], [xa.r])
                yield
                for lv in range(nlev - 1):
                    xb = XT[(lv + 1) % 2]
                    if lv == 0:
                        MMG([(bkn[:, 0:128], xa[:, 2, :], xa[:, 0, :], True, True),
                             (bkn[:, 256:384], xa[:, 0, :], xa[:, 2, :], True, True)], [xa.r], [bkn.r])
                        yield
                        act(xb[:, 0, :], bkn[:, 0:128], AF.Copy, [bkn.r], [xb.r])
                        act(xb[:, 2, :], bkn[:, 256:384], AF.Copy, [bkn.r], [xb.r])
                        gcopy(xb[:, 1, :], xa[:, 1, :], [xa.r], [xb.r])
                    else:
                        MMG([(bkn[:, 0:256], xa[:, 2, :], xa[:, 0:2, :].rearrange("p a b -> p (a b)"), True, True),
                             (bkn[:, 256:384], xa[:, 0, :], xa[:, 2, :], True, True)], [xa.r], [bkn.r])
                        yield
                        act(xb[:, 0, :], bkn[:, 0:128], AF.Copy, [bkn.r], [xb.r])
                        act(xb[:, 2, :], bkn[:, 256:384], AF.Copy, [bkn.r], [xb.r])
                        tt("dve", xb[:, 1, :], bkn[:, 128:256], xa[:, 1, :], ALU.add, [bkn.r, xa.r], [xb.r])
                    yield
                    xa = xb
                MM(bkn[:, 0:128], xa[:, 2, :], xa[:, 1, :], [xa.r], [bkn.r])
                yield
                tt("dve", TmT[:], bkn[:, 0:128], xa[:, 1, :], ALU.add, [bkn.r, xa.r], [TmT.r])
                MM(bkn[:, 384:448], BmT, Vtok, [M1.r, C4.r], [bkn.r])
                yield
                vcopy(BmV[:], bkn[:, 384:448], [bkn.r], [BmV.r])
                yield

            def t_chain(fc, ch, q):
                b_ = ch % 2
                M1, M2, C4, TmT, BmV = q.M1[b_], q.M2[b_], q.C4[b_], q.TmT[b_], q.BmV[b_]
                Xb, Ub, y1, yv, t64, t64b, st6, mv, rs1 = q.Xb, q.Ub, q.y1, q.yv, q.t64, q.t64b, q.st6, q.mv, q.rs1
                bkx = q.bk[3]
                cs = slice(ch * CH, (ch + 1) * CH)
                kap = q.KRbd[:, ch, 0, :]; rti = q.KRbd[:, ch, 1, :]
                CmT = M1[:, 128:256]; nDmT = M2[:, 128:256]
                Vtok = C4[:, 128:192]; kT = C4[:, 192:320]; nbT = C4[:, 320:448]
                if not sample:
                    Hf, Hb = Z.Hf, Z.Hb
                    hb = Hb[:, fc, :]
                    MMG([(bkx[:, 0:64], kap, hb, True, True),
                         (bkx[:, 192:256], rti, hb, True, False),
                         (bkx[:, 192:256], CmT, Vtok, False, True)],
                        [q.KRbd.r, Hb.res[fc], M1.r, C4.r], [bkx.r])
                    yield
                    tt("dve", Xb[:], bkx[:, 0:64], BmV[:], ALU.add, [bkx.r, BmV.r], [Xb.r])
                    act(y1[:], bkx[:, 192:256], AF.Copy, [bkx.r], [y1.r])
                    yield
                else:
                    Hsf, Hsb, Gs = Z.Hsf, Z.Hsb, Z.Gs
                    hall = Hsb[:].rearrange("p s v -> p (s v)")
                    rsel = cf("rowsel").unsqueeze(2).to_broadcast([128, 16, 64])
                    for (lh, dstt) in ((kap, t64), (rti, y1)):
                        for hf in range(2):
                            MM(B[hf][:, :], lh, hall[:, hf * 512:(hf + 1) * 512], [q.KRbd.r, Hsb.r], [B[hf].r])
                        for hf in range(2):
                            tt("dve", Gs[:, hf * 8:(hf + 1) * 8, :],
                               B[hf][:, :].rearrange("p (s v) -> p s v", v=64),
                               rsel[:, hf * 8:(hf + 1) * 8, :], ALU.mult, [B[hf].r, cst.r], [Gs.r])
                        V(lambda e, dstt=dstt: e.tensor_reduce(dstt[:], Gs[:].rearrange("p s v -> p v s"), AX.X, ALU.add),
                          [Gs.r], [dstt.r])
                    tt("dve", Xb[:], t64[:], BmV[:], ALU.add, [t64.r, BmV.r], [Xb.r])
                    MM(bkx[:, 192:256], CmT, Vtok, [M1.r, C4.r], [bkx.r])
                    tt("dve", y1[:], y1[:], bkx[:, 192:256], ALU.add, [y1.r, bkx.r], [y1.r])
                    yield
                MM(bkx[:, 64:128], TmT[:], Xb[:], [TmT.r, Xb.r], [bkx.r])
                yield
                act(Ub[:], bkx[:, 64:128], AF.Copy, [bkx.r], [Ub.r])
                yield
                if not sample:
                    MMG([(bkx[:, 256:320], nDmT, Ub[:], True, True),
                         (bkx[:, 128:192], kT, Vtok, True, False),
                         (bkx[:, 128:192], nbT, Ub[:], False, True)], [M2.r, C4.r, Ub.r], [bkx.r])
                    yield
                    tt("dve", t64[:], bkx[:, 128:192], Hf[:, fc, :], ALU.add, [bkx.r, Hf.res[fc]], [t64.r])
                    dec = q.dec[:, ch:ch + 1]
                    act(Hb[:, fc, :], t64[:], AF.Identity, [t64.r, q.dec.r], [Hb.res[fc]], scale=dec)
                    ts("dve", Hf[:, fc, :], t64[:], dec, None, ALU.mult, None, [t64.r, q.dec.r], [Hf.res[fc]])
                    tt("dve", yv[:], bkx[:, 256:320], y1[:], ALU.add, [bkx.r, y1.r], [yv.r])
                    yield
                else:
                    MM(bkx[:, 256:320], nDmT, Ub[:], [M2.r, Ub.r], [bkx.r])
                    tt("dve", yv[:], bkx[:, 256:320], y1[:], ALU.add, [bkx.r, y1.r], [yv.r])
                    vsel, usel, sw_out = Z.vsel, Z.usel, Z.sw_out
                    rsel = cf("rowsel").unsqueeze(2).to_broadcast([128, 16, 64])
                    tt("dve", vsel[:], Vtok.unsqueeze(1).to_broadcast([128, 16, 64]), rsel, ALU.mult,
                       [C4.r, cst.r], [vsel.r])
                    tt("dve", usel[:], Ub[:].unsqueeze(1).to_broadcast([128, 16, 64]), rsel, ALU.mult,
                       [Ub.r, cst.r], [usel.r])
                    for hf in range(2):
                        MMG([(B[hf][:, :], kT, vsel[:, hf * 8:(hf + 1) * 8, :].rearrange("p s v -> p (s v)"), True, False),
                             (B[hf][:, :], nbT, usel[:, hf * 8:(hf + 1) * 8, :].rearrange("p s v -> p (s v)"), False, True)],
                            [C4.r, vsel.r, usel.r], [B[hf].r])
                    decs = q.dec[:, 0:16].unsqueeze(2).to_broadcast([128, 16, 64])
                    for hf in range(2):
                        tt("dve", Hsf[:, hf * 8:(hf + 1) * 8, :], Hsf[:, hf * 8:(hf + 1) * 8, :],
                           B[hf][:, :].rearrange("p (s v) -> p s v", v=64), ALU.add,
                           [B[hf].r, Hsf.r], [Hsf.r])
                    tt("dve", Hsf[:], Hsf[:], decs, ALU.mult, [Hsf.r, q.dec.r], [Hsf.r])
                    for qq in range(4):
                        for j in range(4):
                            TR(B[4][0:64, j * 128:(j + 1) * 128], Hsf[:, 4 * qq + j, :], ident,
                               [Hsf.r, cst.r], [B[4].r])
                        vcopy(sw_out[:, 4 * qq:4 * qq + 4, :, :].rearrange("v s h k -> v s (h k)"),
                              B[4][0:64, :].rearrange("v (s x) -> v s x", x=128), [B[4].r], sw_out.res[4 * qq:4 * qq + 4])
                        for s_ in range(4 * qq, 4 * qq + 4):
                            S.dma("sp", wkvs_d[l, s_, 2 * fc:2 * fc + 2].rearrange("h v k -> v h k"),
                                  sw_out[:, s_, :, :], [sw_out.res[s_]], [])
                    yield
                V(lambda e: e.bn_stats(st6[:], yv[:]), [yv.r], [st6.r])
                V(lambda e: e.bn_aggr(mv[:], st6[:]), [st6.r], [mv.r])
                yield
                rstd_from_ss(mv[:, 1:2], rs1[:, 0:1], 1.0, 64e-5, [mv.r], [rs1.r])
                yield
                for hh in range(2):
                    pp = slice(hh * 64, (hh + 1) * 64)
                    ts("dve", q.ynbd[pp, pp], yv[pp, :], mv[pp, 0:1], rs1[pp, 0:1], ALU.subtract, ALU.mult,
                       [yv.r, mv.r, rs1.r], [q.ynbd.r])
                MM(bkx[:, 384:448], q.ynbd[:], iselb, [q.ynbd.r, cstb.r], [bkx.r])
                yield
                act(t64b[:], bkx[:, 384:448], AF.Identity, [bkx.r, prm.r], [t64b.r],
                    scale=pcol("gn_r_g", fc), bias=pcol("gn_r_b", fc))
                yield
                tt("pool", t64b[:], t64b[:], q.bv[:, cs], ALU.add, [t64b.r, q.bv.r], [t64b.r])
                tt("pool", ymix[:, fc, cs], t64b[:], q.sgr[:, cs], ALU.mult, [t64b.r, q.sgr.r], [ymix.res[fc]])
                yield

            tasks = []
            nset = len(Z.sets)
            for fc in range(4):
                q = Z.sets[fc % nset]
                deps = []
                if fc >= 1:
                    deps.append("prep%d" % (fc - 1))
                if fc >= nset:
                    deps.append("chain%d_%d" % (fc - nset, nchunk - 1))
                tasks.append(("prep%d" % fc, (lambda fc=fc, q=q: t_prep(fc, q)), deps))
                for ch in range(nchunk):
                    d1 = ["prep%d" % fc]
                    if ch >= 1:
                        d1.append("pre%d_%d" % (fc, ch - 1))
                    if ch >= 2:
                        d1.append("chain%d_%d" % (fc, ch - 2))
                    tasks.append(("pre%d_%d" % (fc, ch), (lambda fc=fc, ch=ch, q=q: t_pre(fc, ch, q)), d1))
                    d2 = ["pre%d_%d" % (fc, ch)]
                    if ch >= 1:
                        d2.append("chain%d_%d" % (fc, ch - 1))
                    tasks.append(("chain%d_%d" % (fc, ch), (lambda fc=fc, ch=ch, q=q: t_chain(fc, ch, q)), d2))
            run_tasks(tasks + list(extra_tasks))

        def conv_tile(l, W, sample):
            tmp, tb, dg, ymix = Z.tmp, Z.tb, Z.dg, Z.ymix
            for cc in range(4):
                project(17 + cc, W, B[2]); project(21 + cc, W, B[3])
                eb = tmp[0]
                act(eb[:, 0:W], B[3][:, 0:W], AF.Exp, [B[3].r], [eb.r], scale=-1.0)
                sigmoid_from_exp(eb[:, 0:W], eb.r)
                if not sample:
                    cbf, cbb = Z.cbf, Z.cbb
                    tt("dve", cbf[:, cc, 30:30 + W], B[2][:, 0:W], eb[:, 0:W], ALU.mult, [B[2].r, eb.r], [cbf.res[cc]])
                    gcopy(cbb[:, cc, 30:30 + W], cbf[:, cc, 30:30 + W], [cbf.res[cc]], [cbb.res[cc]])
                else:
                    cbs_f, cbs_b = Z.cbs_f, Z.cbs_b
                    tt("dve", cbs_f[:, cc, :, 30:34], v3(B[2][:, 0:W]), v3(eb[:, 0:W]), ALU.mult,
                       [B[2].r, eb.r], [cbs_f.res[cc]])
                    vcopy(cbs_b[:, cc, :, :], cbs_f[:, cc, :, :], [cbs_f.res[cc]], [cbs_b.res[cc]])
                for (j0, j1) in ((0, 16), (16, 31)):
                    nj = j1 - j0
                    wd = prm[:, PO["wdw"] + cc * 31 + j0:PO["wdw"] + cc * 31 + j1]
                    tt("dve", dg[:, 0:nj, :], identb.unsqueeze(1).to_broadcast([128, nj, 128]),
                       wd.unsqueeze(2).to_broadcast([128, nj, 128]), ALU.mult, [cstb.r, prm.r], [dg.r])
                    if not sample:
                        MMG([(B[6][:, 0:W], dg[:, j - j0, :], cbb[:, cc, j:j + W], j == 0, j == 30) for j in range(j0, j1)],
                            [dg.r, cbb.res[cc]], [B[6].r])
                    else:
                        MMG([(v3(B[6][:, 0:W]), dg[:, j - j0, :], cbs_b[:, cc, :, j:j + 4], j == 0, j == 30)
                             for j in range(j0, j1)], [dg.r, cbs_b.res[cc]], [B[6].r])
                cf_, cb_, sq_ = tmp[1], tb[0], tb[1]
                act(cf_[:, 0:W], B[6][:, 0:W], AF.Identity, [B[6].r, prm.r], [cf_.r], bias=pcol("b_dw", cc))
                act(cb_[:, 0:W], B[6][:, 0:W], AF.Identity, [B[6].r, prm.r], [cb_.r], bias=pcol("b_dw", cc))
                act(sq_[:, 0:W], B[6][:, 0:W], AF.Square, [B[6].r, prm.r], [sq_.r], bias=pcol("b_dw", cc))
                MM(B[4][:, 0:W], bonesb, cb_[:, 0:W], [cstb.r, cb_.r], [B[4].r])
                MM(B[5][:, 0:W], bonesb, sq_[:, 0:W], [cstb.r, sq_.r], [B[5].r])
                msq, var, xc = tmp[2], tmp[3], tmp[4]
                act(msq[:, 0:W], B[4][:, 0:W], AF.Square, [B[4].r], [msq.r], scale=1.0 / 64)
                stt(var[:, 0:W], B[5][:, 0:W], 1.0 / 64, msq[:, 0:W], ALU.mult, ALU.subtract, [B[5].r, msq.r], [var.r])
                rstd_from_ss(var[:, 0:W], var[:, 0:W], 1.0, 1e-5, [var.r], [var.r])
                stt(xc[:, 0:W], B[4][:, 0:W], -1.0 / 64, cf_[:, 0:W], ALU.mult, ALU.add, [B[4].r, cf_.r], [xc.r])
                tt("pool", xc[:, 0:W], xc[:, 0:W], var[:, 0:W], ALU.mult, [xc.r, var.r], [xc.r])
                gnv, en = tmp[5], tmp[6]
                act(gnv[:, 0:W], xc[:, 0:W], AF.Identity, [xc.r, prm.r], [gnv.r],
                    scale=pcol("gn_c_g", cc), bias=pcol("gn_c_b", cc))
                act(en[:, 0:W], xc[:, 0:W], AF.Exp, [xc.r, drv.r], [en.r], scale=dcol(29 + cc), bias=dcol(33 + cc))
                sigmoid_from_exp(en[:, 0:W], en.r)
                tt("pool", gnv[:, 0:W], gnv[:, 0:W], en[:, 0:W], ALU.mult, [gnv.r, en.r], [gnv.r])
                project(25 + cc, W, B[2])
                eg = tmp[7]
                act(eg[:, 0:W], B[2][:, 0:W], AF.Exp, [B[2].r], [eg.r], scale=-1.0)
                sigmoid_from_exp(eg[:, 0:W], eg.r)
                tt("dve", eg[:, 0:W], eg[:, 0:W], B[2][:, 0:W], ALU.mult, [eg.r, B[2].r], [eg.r])
                tt("pool", ymix[:, 4 + cc, 0:W], gnv[:, 0:W], eg[:, 0:W], ALU.mult, [gnv.r, eg.r], [ymix.res[4 + cc]])
                if not sample:
                    gcopy(cbf[:, cc, 0:30], cbf[:, cc, W:W + 30], [cbf.res[cc]], [cbf.res[cc]])
                    gcopy(cbb[:, cc, 0:30], cbb[:, cc, W:W + 30], [cbb.res[cc]], [cbb.res[cc]])

        def conv_pre(l, W):
            tmp, cbf = Z.tmp, Z.cbf
            for cc in range(4):
                project(17 + cc, W, B[2]); project(21 + cc, W, B[3])
                eb = tmp[0]
                act(eb[:, 0:W], B[3][:, 0:W], AF.Exp, [B[3].r], [eb.r], scale=-1.0)
                sigmoid_from_exp(eb[:, 0:W], eb.r)
                tt("dve", cbf[:, cc, 30:30 + W], B[2][:, 0:W], eb[:, 0:W], ALU.mult, [B[2].r, eb.r], [cbf.res[cc]])

        def conv_tasks(l, W):
            cbf, cacc, accP, tmpP = Z.cbf, Z.cacc, Z.caccP, Z.ctmpP2
            wcol = lambda cc, j: prm[:, PO["wdw"] + cc * 31 + j:PO["wdw"] + cc * 31 + j + 1]

            NDV = 13

            def t_conv(cc):
                accD = cacc[:, cc, 0:W]
                rD = cacc.res[cc]
                ts("dve", accD, cbf[:, cc, 0:W], wcol(cc, 0), pcol("b_dw", cc), ALU.mult, ALU.add,
                   [cbf.res[cc], prm.r], [rD])
                act(accP[:, 0:W], cbf[:, cc, NDV:NDV + W], AF.Identity, [cbf.res[cc], prm.r], [accP.r],
                    scale=wcol(cc, NDV))
                yield
                jd, jp = 1, NDV + 1
                k_ = 0
                while jd < NDV or jp <= 30:
                    if jd < NDV:
                        stt(accD, cbf[:, cc, jd:jd + W], wcol(cc, jd), accD, ALU.mult, ALU.add,
                            [cbf.res[cc], prm.r, rD], [rD])
                        jd += 1
                    if jp <= 30:
                        tb_ = tmpP[k_ % 2]
                        act(tb_[:, 0:W], cbf[:, cc, jp:jp + W], AF.Identity, [cbf.res[cc], prm.r], [tb_.r],
                            scale=wcol(cc, jp))
                        tt("pool", accP[:, 0:W], accP[:, 0:W], tb_[:, 0:W], ALU.add, [accP.r, tb_.r], [accP.r])
                        jp += 1
                        k_ += 1
                    yield
                tt("pool", accD, accD, accP[:, 0:W], ALU.add, [rD, accP.r], [rD])
                yield
            tasks = []
            for cc in range(4):
                tasks.append(("conv%d" % cc, (lambda cc=cc: t_conv(cc)), (["conv%d" % (cc - 1)] if cc else [])))
            return tasks

        def conv_post(l, W):
            tmp, tb, ymix, cbf, cacc = Z.tmp, Z.tb, Z.ymix, Z.cbf, Z.cacc
            for cc in range(4):
                cf_ = cacc[:, cc, 0:W]
                rC = cacc.res[cc]
                cb_, sq_ = tb[0], tb[1]
                act(cb_[:, 0:W], cf_, AF.Copy, [rC], [cb_.r])
                act(sq_[:, 0:W], cf_, AF.Square, [rC], [sq_.r])
                MM(B[4][:, 0:W], bonesb, cb_[:, 0:W], [cstb.r, cb_.r], [B[4].r])
                MM(B[5][:, 0:W], bonesb, sq_[:, 0:W], [cstb.r, sq_.r], [B[5].r])
                msq, var, xc = tmp[2], tmp[3], tmp[4]
                act(msq[:, 0:W], B[4][:, 0:W], AF.Square, [B[4].r], [msq.r], scale=1.0 / 64)
                stt(var[:, 0:W], B[5][:, 0:W], 1.0 / 64, msq[:, 0:W], ALU.mult, ALU.subtract, [B[5].r, msq.r], [var.r])
                rstd_from_ss(var[:, 0:W], var[:, 0:W], 1.0, 1e-5, [var.r], [var.r])
                stt(xc[:, 0:W], B[4][:, 0:W], -1.0 / 64, cf_, ALU.mult, ALU.add, [B[4].r, rC], [xc.r])
                tt("pool", xc[:, 0:W], xc[:, 0:W], var[:, 0:W], ALU.mult, [xc.r, var.r], [xc.r])
                gnv, en = tmp[5], tmp[6]
                act(gnv[:, 0:W], xc[:, 0:W], AF.Identity, [xc.r, prm.r], [gnv.r],
                    scale=pcol("gn_c_g", cc), bias=pcol("gn_c_b", cc))
                act(en[:, 0:W], xc[:, 0:W], AF.Exp, [xc.r, drv.r], [en.r], scale=dcol(29 + cc), bias=dcol(33 + cc))
                sigmoid_from_exp(en[:, 0:W], en.r)
                tt("pool", gnv[:, 0:W], gnv[:, 0:W], en[:, 0:W], ALU.mult, [gnv.r, en.r], [gnv.r])
                project(25 + cc, W, B[2])
                eg = tmp[7]
                act(eg[:, 0:W], B[2][:, 0:W], AF.Exp, [B[2].r], [eg.r], scale=-1.0)
                sigmoid_from_exp(eg[:, 0:W], eg.r)
                tt("dve", eg[:, 0:W], eg[:, 0:W], B[2][:, 0:W], ALU.mult, [eg.r, B[2].r], [eg.r])
                tt("pool", ymix[:, 4 + cc, 0:W], gnv[:, 0:W], eg[:, 0:W], ALU.mult, [gnv.r, eg.r], [ymix.res[4 + cc]])
                gcopy(cbf[:, cc, 0:30], cbf[:, cc, W:W + 30], [cbf.res[cc]], [cbf.res[cc]])

        def outproj(l, sample, subs, store):
            junk, sm, xh, ymix = Z.junk, Z.sm, Z.xh, Z.ymix
            for si in range(len(subs)):
                xt = Z.xt2[si % len(Z.xt2)]
                loader, xsrc, xr, npart = subs[si]
                MMG([(B[hf][0:npart, :], ymix[:, kc, si * 128:si * 128 + npart], wout[:, kc, hf * 512:(hf + 1) * 512],
                      kc == 0, kc == 7) for hf in range(2) for kc in range(8)],
                    [wout.r] + ymix.res, [B[0].r, B[1].r])
                act(junk[0:npart, :], B[0][0:npart, :], AF.Square, [B[0].r], [junk.r, sm.r], accum=sm[0:npart, 8:9])
                act(junk[0:npart, :], B[1][0:npart, :], AF.Square, [B[1].r], [junk.r, sm.r], accum=sm[0:npart, 9:10])
                tt("dve", sm[0:npart, 10:11], sm[0:npart, 8:9], sm[0:npart, 9:10], ALU.add, [sm.r], [sm.r])
                rstd_from_ss(sm[0:npart, 10:11], sm[0:npart, 11:12], 1.0 / D, 1e-6, [sm.r], [sm.r])
                if loader is not None:
                    loader(xt)
                    xsrc, xr = xt[0:npart, :], xt.r
                gg = ggs if sample else ggp
                for hf in range(2):
                    cs = slice(hf * 512, (hf + 1) * 512)
                    stt(xh[0:npart, :], B[hf][0:npart, :], sm[0:npart, 11:12], gg[0:npart, cs], ALU.mult, ALU.mult,
                        [B[hf].r, sm.r, gg.r], [xh.r])
                    tt("dve", xsrc[:, cs], xh[0:npart, :], xsrc[:, cs], ALU.add, [xh.r, xr], [xr])
                store(si, npart, xsrc, xr)

        S.dma("sp", xs[:], xs_d[:, :], [], [xs.r])
        for l in range(DEPTH):
            areset()
            load_layer(l)
            prologue(l, l == 0)
            load_big(l)
            if stop <= 0:
                break
            areset()
            carve(TS, True)
            scl, cbs_f = Z.scl, Z.cbs_f
            for g4 in range(4):
                S.dma("sp", scl[:, g4, :], scv_d[l, g4 * 120:(g4 + 1) * 120, :], [], [scl.r])
            for cc in range(4):
                for g4 in range(4):
                    TR(B[5][:, g4 * 120:(g4 + 1) * 120], scl[0:120, g4, cc * 128:(cc + 1) * 128],
                       ident[0:120, 0:120], [scl.r, cst.r], [B[5].r])
                vcopy(cbs_f[:, cc, :, 0:30], B[5][:, 0:480].rearrange("p (s j) -> p s j", j=30),
                      [B[5].r], [cbs_f.res[cc]])
            xs_subs = [(None, xs[0:TS, :], xs.r, TS)]
            stage1(l, xs_subs, True, False)
            rwkv_tile(l, TS, True)
            conv_tile(l, TS, True)

            def store_s(si, npart, xsrc, xr, l=l):
                if l == DEPTH - 1:
                    S.dma("sp", ys_d[:, :], xs[0:TS, :], [xs.r], [])
            outproj(l, True, xs_subs, store_s)
            for cc in range(4):
                for g4 in range(4):
                    t8 = Z.tmp[7]
                    vcopy(t8[:, 0:120].rearrange("p (s j) -> p s j", j=30), cbs_f[:, cc, g4 * 4:(g4 + 1) * 4, 4:34],
                          [cbs_f.res[cc]], [t8.r])
                    TR(B[5][0:120, 0:128], t8[:, 0:120], ident, [t8.r, cst.r], [B[5].r])
                    vcopy(scl[0:120, g4, cc * 128:(cc + 1) * 128], B[5][0:120, 0:128], [B[5].r], [scl.r])
            for g4 in range(4):
                S.dma("sp", cvs_d[l, g4 * 120:(g4 + 1) * 120, :], scl[:, g4, :], [scl.r], [])
            if stop <= 1:
                break

            areset()
            carve(TT, False)
            Hf, Hb, carry, cbf, xh = Z.Hf, Z.Hb, Z.carry, Z.cbf, Z.xh
            G(lambda e, t_=Hf: e.memset(t_[:], 0.0), [], Hf.res)
            G(lambda e, t_=Hb: e.memset(t_[:], 0.0), [], Hb.res)
            G(lambda e, t_=carry: e.memset(t_[:], 0.0), [], [carry.r])
            for cc in range(4):
                G(lambda e, cc=cc, t_=cbf: e.memset(t_[:, cc, 0:30], 0.0), [], [cbf.res[cc]])
            src_d = xp_d if l == 0 else xmid_d
            dst_d = yp_d if l == DEPTH - 1 else xmid_d
            for ti in range(NT):
                subs = []
                for si in range(2):
                    r0 = ti * TT + si * 128

                    def loader(xt_, r0=r0, src_d=src_d):
                        S.dma("sp", xt_[:, :], src_d[r0:r0 + 128, :], [], [xt_.r])
                    subs.append((loader, None, None, 128))
                stage1(l, subs, False, ti == NT - 1)
                conv_pre(l, TT)
                rwkv_tile(l, TT, False, conv_tasks(l, TT))
                conv_post(l, TT)

                def store_p(si, npart, xsrc, xr, ti=ti, dst_d=dst_d):
                    r0 = ti * TT + si * 128
                    S.dma("sp", dst_d[r0:r0 + 128, :], xsrc, [xr], [])
                outproj(l, False, subs, store_p)
                if stop <= 2:
                    break
            if stop <= 3:
                break
            for fc in range(4):
                TR(B[4][0:64, 0:128], Hf[:, fc, :], ident, [Hf.res[fc], cst.r], [B[4].r])
                vcopy(xh[0:64, fc * 128:(fc + 1) * 128], B[4][0:64, 0:128], [B[4].r], [xh.r])
            S.dma("sp", wkvp_d[l].rearrange("h v k -> v h k"), xh[0:64, :].rearrange("v (h k) -> v h k", k=64), [xh.r], [])
            for cc in range(4):
                TR(B[5][0:30, cc * 128:(cc + 1) * 128], cbf[:, cc, 0:30], ident, [cbf.res[cc], cst.r], [B[5].r])
            vcopy(xh[0:30, :], B[5][0:30, :], [B[5].r], [xh.r])
            S.dma("sp", cvp_d[l], xh[0:30, :], [xh.r], [])
        S.barrier()

        print("arena max use", amax)
        with nc.Block() as block:
            S.emit(block, None)
    return nc


_NC = None


def kernel(**inp):
    global _NC
    inp = {k: np.asarray(v) for k, v in inp.items()}
    if _NC is None:
        _NC = build()
    f32 = lambda a: np.ascontiguousarray(a, dtype=np.float32)
    prm = f32(np.stack([pack_params(inp, l) for l in range(DEPTH)]))
    prow = np.zeros((DEPTH, 17, 5 * D), np.float32)
    for l in range(DEPTH):
        prow[l, :, 0:3 * D] = inp["b_ada"][l][None]
        prow[l, :, 3 * D:4 * D] = inp["g_pre"][l][None]
        prow[l, :, 4 * D:5 * D] = inp["g_post"][l][None]
    lora = np.zeros((DEPTH, 128, 1024), np.float32)
    lora[:, 0:64, 0:512] = inp["w_up"]
    lora[:, 64:128, 512:1024] = inp["a_up"]
    in_maps = []
    for i in range(8):
        sl = slice(NSEQ * i, NSEQ * (i + 1))
        in_maps.append({
            "xp": f32(inp["x_prompt"][i]), "xs": f32(inp["x_sample"][sl].reshape(TS, D)),
            "cc": f32(np.concatenate([inp["c_prompt"][i:i + 1], inp["c_sample"][sl]], 0)),
            "ssh": f32(inp["state_shift"][:, sl]), "swkv": f32(inp["state_wkv"][:, sl]),
            "scv": f32(inp["state_conv"][:, sl].reshape(DEPTH, NSEQ * 30, 512)),
            "w_ada": f32(inp["w_ada"]), "w_in": f32(inp["w_in"]), "w_out": f32(inp["w_out"]),
            "prm": prm, "prow": prow, "lora": lora, "cst": CST, "cste": CSTE,
        })
    res = run_bass_kernel_spmd(_NC, in_maps, core_ids=list(range(8)))
    R = res.results
    cat = lambda k: np.stack([R[i][k] for i in range(8)])
    y_p = cat("yp")
    y_s = cat("ys").reshape(128, 4, D)
    sh_p = cat("shp").transpose(1, 0, 2)
    wkv_p = cat("wkvp").transpose(1, 0, 2, 3, 4)
    cv_p = cat("cvp").transpose(1, 0, 2, 3)
    sh_s = cat("shs").transpose(1, 0, 2, 3).reshape(DEPTH, 128, D)
    wkv_s = cat("wkvs").transpose(1, 0, 2, 3, 4, 5).reshape(DEPTH, 128, 8, 64, 64)
    cv_s = cat("cvs").reshape(8, DEPTH, NSEQ, 30, 512).transpose(1, 0, 2, 3, 4).reshape(DEPTH, 128, 30, 512)
    return tuple(np.ascontiguousarray(a, dtype=np.float32) for a in
                 (y_p, y_s, sh_p, wkv_p, cv_p, sh_s, wkv_s, cv_s))
```
